# Optimizing a Trainium2 kernel written in Bass

```python
import jax, jax.numpy as jnp
from jax import lax
import numpy as np

D_MODEL = 2048
BATCH = 2
SEQ = 16384
DEPTH = 2

CONV_CH = D_MODEL // 2
CONV_WIDTH = 31
NSA_HEADS = 16
NSA_HEAD_DIM = 64
NSA_KV_GROUPS = 4
NSA_HPG = NSA_HEADS // NSA_KV_GROUPS
NSA_Q = NSA_HEADS * NSA_HEAD_DIM
NSA_KV = NSA_KV_GROUPS * NSA_HEAD_DIM
CMP_STRIDE = 16
CMP_BLOCK = 2 * CMP_STRIDE
CMP_HIDDEN = 256
SLC_BLOCK = 64
SLC_TOPK = 16
WINDOW = 512
Q_BLOCK = 128
FORCE_SCORE = 1.0e3
MIX_WIDTH = CONV_CH + NSA_Q
EVEN_SPLITS = (2 * CONV_CH, NSA_Q, NSA_KV, NSA_KV, NSA_KV, NSA_KV, NSA_KV, NSA_KV, 3 * NSA_HEADS)
EVEN_IN = 2 * CONV_CH + NSA_Q + 6 * NSA_KV + 3 * NSA_HEADS
HG_HEADS = 16
HG_DK = 128
HG_DV = D_MODEL // HG_HEADS
HG_CHUNK = 64
ODD_SPLITS = (HG_HEADS * HG_DK, HG_HEADS * HG_DK, HG_HEADS * HG_DV, HG_HEADS * HG_DV)
ODD_IN = 2 * HG_HEADS * HG_DK + 2 * HG_HEADS * HG_DV
FFN_HIDDEN = ((8 * D_MODEL // 3 + 255) // 256) * 256
N_EVEN = (DEPTH + 1) // 2
N_ODD = DEPTH // 2
EPS = 1e-6
TINY = 1e-30

kernel_name = "hybrid_conv_nsa_hgrn2_trunk"


def _split_cols(u, sizes):
    offs = np.cumsum(np.array(sizes))[:-1].tolist()
    return jnp.split(u, offs, axis=-1)


def rmsnorm(x, w):
    xf = x.astype(jnp.float32)
    y = xf * lax.rsqrt(jnp.mean(xf * xf, axis=-1, keepdims=True) + EPS)
    return (y * w.astype(jnp.float32)).astype(x.dtype)


def alibi_slopes(n_heads):
    return 2.0 ** (-8.0 * jnp.arange(1, n_heads + 1, dtype=jnp.float32) / n_heads)


def masked_softmax(s, mask):
    s = jnp.where(mask, s, -jnp.inf)
    m = jnp.max(s, axis=-1, keepdims=True)
    m = jnp.where(jnp.isfinite(m), m, 0.0)
    p = jnp.exp(s - m)
    return p / jnp.maximum(jnp.sum(p, axis=-1, keepdims=True), TINY)


def conformer_conv(u, conv_w, conv_b, ln_w, ln_b):
    a, g = jnp.split(u, 2, axis=-1)
    h = a * jax.nn.sigmoid(g)
    h = lax.conv_general_dilated(h, conv_w.astype(h.dtype), window_strides=(1,),
                                 padding=[(CONV_WIDTH - 1, 0)],
                                 dimension_numbers=('NWC', 'WIO', 'NWC'),
                                 feature_group_count=CONV_CH) + conv_b
    hf = h.astype(jnp.float32)
    mu = jnp.mean(hf, axis=-1, keepdims=True)
    var = jnp.mean(jnp.square(hf - mu), axis=-1, keepdims=True)
    hn = (hf - mu) * lax.rsqrt(var + EPS) * ln_w + ln_b
    return jax.nn.silu(hn).astype(u.dtype)


def compress_blocks(kv, pos, w1, b1, w2, b2):
    B, S, G, Dh = kv.shape
    halves = kv.reshape(B, S // CMP_STRIDE, CMP_STRIDE, G, Dh)
    blocks = jnp.concatenate([halves[:, :-1], halves[:, 1:]], axis=2)
    blocks = blocks + pos[None, None, :, None, :]
    flat = blocks.transpose(0, 1, 3, 2, 4).reshape(B, S // CMP_STRIDE - 1, G, CMP_BLOCK * Dh)
    return jax.nn.silu(flat @ w1 + b1) @ w2 + b2


def nsa_attention(q, k_cmp, v_cmp, k_slc, v_slc, k_win, v_win, gates):
    f32 = jnp.float32
    B, S = q.shape[:2]
    G, HPG, DH = NSA_KV_GROUPS, NSA_HPG, NSA_HEAD_DIM
    n_cmp = S // CMP_STRIDE - 1
    n_slc = S // SLC_BLOCK
    n_top = min(SLC_TOPK, n_slc)
    slopes = alibi_slopes(NSA_HEADS).reshape(G, HPG)
    q = q.astype(f32) * DH ** -0.5
    k_cmp = k_cmp.astype(f32)
    v_cmp = v_cmp.astype(f32)
    cmp_end = jnp.arange(n_cmp) * CMP_STRIDE + (CMP_BLOCK - 1)
    ci = jnp.arange(n_cmp)[:, None] * CMP_STRIDE
    sj = jnp.arange(n_slc)[None, :] * SLC_BLOCK
    overlap = ((ci < sj + SLC_BLOCK) & (ci + CMP_BLOCK > sj)).astype(f32)

    def to_blocks(t):
        return t.astype(f32).reshape(B, n_slc, SLC_BLOCK, G, DH).transpose(0, 3, 1, 2, 4).reshape(
            B, G, n_slc, SLC_BLOCK * DH)

    ks_blk, vs_blk = to_blocks(k_slc), to_blocks(v_slc)
    gather = jax.vmap(jax.vmap(lambda blk, ix: blk[ix]))
    pad = ((0, 0), (WINDOW, 0), (0, 0), (0, 0))
    kw_pad = jnp.pad(k_win.astype(f32), pad)
    vw_pad = jnp.pad(v_win.astype(f32), pad)
    nk = n_top * SLC_BLOCK

    def block(qb):
        t0 = qb * Q_BLOCK
        t = t0 + jnp.arange(Q_BLOCK)
        qblk = lax.dynamic_slice_in_dim(q, t0, Q_BLOCK, axis=1)
        gblk = lax.dynamic_slice_in_dim(gates, t0, Q_BLOCK, axis=1)
        dist = (t[:, None] - cmp_end[None, :]).astype(f32)
        s = jnp.einsum('bqghd,bngd->bghqn', qblk, k_cmp) - slopes[None, :, :, None, None] * dist
        p_cmp = masked_softmax(s, dist >= 0)
        o_cmp = jnp.einsum('bghqn,bngd->bqghd', p_cmp, v_cmp)
        imp = jnp.einsum('bgqn,nj->bgqj', jnp.sum(p_cmp, axis=2), overlap)
        cur = (t // SLC_BLOCK)[:, None]
        j = jnp.arange(n_slc)[None, :]
        forced = (j == 0) | (j == cur) | (j == cur - 1)
        imp = jnp.where(forced, FORCE_SCORE, jnp.where(j > cur, -1.0, imp))
        _, idx = lax.top_k(imp, n_top)
        flat_idx = idx.reshape(B, G, Q_BLOCK * n_top)
        k_sel = gather(ks_blk, flat_idx).reshape(B, G, Q_BLOCK, nk, DH)
        v_sel = gather(vs_blk, flat_idx).reshape(B, G, Q_BLOCK, nk, DH)
        pos = (idx[..., None] * SLC_BLOCK + jnp.arange(SLC_BLOCK)).reshape(B, G, Q_BLOCK, nk)
        dist = (t[None, None, :, None] - pos).astype(f32)
        s = jnp.einsum('bqghd,bgqmd->bghqm', qblk, k_sel) - slopes[None, :, :, None, None] * dist[:, :, None]
        p = masked_softmax(s, (dist >= 0)[:, :, None])
        o_slc = jnp.einsum('bghqm,bgqmd->bqghd', p, v_sel)
        kwin = lax.dynamic_slice_in_dim(kw_pad, t0, WINDOW + Q_BLOCK, axis=1)
        vwin = lax.dynamic_slice_in_dim(vw_pad, t0, WINDOW + Q_BLOCK, axis=1)
        kpos = t0 - WINDOW + jnp.arange(WINDOW + Q_BLOCK)
        dist = t[:, None] - kpos[None, :]
        mask = (dist >= 0) & (dist < WINDOW) & (kpos[None, :] >= 0)
        s = jnp.einsum('bqghd,bkgd->bghqk', qblk, kwin) - slopes[None, :, :, None, None] * dist.astype(f32)
        p = masked_softmax(s, mask)
        o_win = jnp.einsum('bghqk,bkgd->bqghd', p, vwin)
        o = (gblk[:, :, 0, :, :, None] * o_cmp + gblk[:, :, 1, :, :, None] * o_slc
             + gblk[:, :, 2, :, :, None] * o_win)
        return o.reshape(B, Q_BLOCK, NSA_Q)

    out = lax.map(block, jnp.arange(S // Q_BLOCK))
    return out.transpose(1, 0, 2, 3).reshape(B, S, NSA_Q)


def even_mixer(h, w_in, conv_w, conv_b, ln_w, ln_b, cmp_pos, cmp_w1, cmp_b1, cmp_w2, cmp_b2, w_out):
    B, S, _ = h.shape
    u = h @ w_in
    a_in, q, kc, vc, ks, vs, kw, vw, g = _split_cols(u, EVEN_SPLITS)
    a_out = conformer_conv(a_in, conv_w, conv_b, ln_w, ln_b)

    def kv(t):
        return t.reshape(B, S, NSA_KV_GROUPS, NSA_HEAD_DIM)

    k_cmp = compress_blocks(kv(kc), cmp_pos[0], cmp_w1[0], cmp_b1[0], cmp_w2[0], cmp_b2[0])
    v_cmp = compress_blocks(kv(vc), cmp_pos[1], cmp_w1[1], cmp_b1[1], cmp_w2[1], cmp_b2[1])
    gates = jax.nn.sigmoid(g.astype(jnp.float32)).reshape(B, S, 3, NSA_KV_GROUPS, NSA_HPG)
    b_out = nsa_attention(q.reshape(B, S, NSA_KV_GROUPS, NSA_HPG, NSA_HEAD_DIM), k_cmp, v_cmp,
                          kv(ks), kv(vs), kv(kw), kv(vw), gates).astype(h.dtype)
    return jnp.concatenate([a_out, b_out], axis=-1) @ w_out


def odd_mixer(h, w_in, lb, gnorm_w, w_out):
    f32 = jnp.float32
    B, S, _ = h.shape
    q, f_logit, i, g = _split_cols(h @ w_in, ODD_SPLITS)
    f = lb + (1.0 - lb) * jax.nn.sigmoid(f_logit.astype(f32))
    log_f = jnp.log(jnp.maximum(f, TINY))
    k = 1.0 - f
    n_chunk = S // HG_CHUNK

    def chunks(t, d):
        return t.astype(f32).reshape(B, n_chunk, HG_CHUNK, HG_HEADS, d).transpose(1, 0, 3, 2, 4)

    xs = (chunks(q, HG_DK), chunks(k, HG_DK), chunks(log_f, HG_DK), chunks(i, HG_DV))
    causal = jnp.tril(jnp.ones((HG_CHUNK, HG_CHUNK), dtype=bool))[:, :, None]

    def step(state, inp):
        qc, kc, gc, ic = inp
        Gc = jnp.cumsum(gc, axis=2)
        o_inter = jnp.einsum('bhtk,bhkv->bhtv', qc * jnp.exp(Gc), state)
        decay = jnp.exp(jnp.where(causal, Gc[:, :, :, None, :] - Gc[:, :, None, :, :], -jnp.inf))
        a = jnp.einsum('bhtk,bhsk,bhtsk->bhts', qc, kc, decay)
        o_intra = jnp.einsum('bhts,bhsv->bhtv', a, ic)
        g_last = Gc[:, :, -1, :]
        state = (jnp.exp(g_last)[..., None] * state
                 + jnp.einsum('bhsk,bhsv->bhkv', kc * jnp.exp(g_last[:, :, None, :] - Gc), ic))
        return state, o_inter + o_intra

    s0 = jnp.zeros((B, HG_HEADS, HG_DK, HG_DV), f32)
    _, o = lax.scan(step, s0, xs)
    o = o.transpose(1, 0, 3, 2, 4).reshape(B, S, HG_HEADS, HG_DV)
    o = o * lax.rsqrt(jnp.mean(o * o, axis=-1, keepdims=True) + EPS) * gnorm_w.astype(f32).reshape(HG_HEADS, HG_DV)
    o = (o.reshape(B, S, HG_HEADS * HG_DV) * jax.nn.silu(g.astype(f32))).astype(h.dtype)
    return o @ w_out


def swiglu(h, w_gu, w_down):
    a, b = jnp.split(h @ w_gu, 2, axis=-1)
    return (jax.nn.silu(a) * b) @ w_down


def setup_inputs(seed: int = 0) -> dict:
    key = jax.random.key(seed)
    ks = jax.random.split(key, 20)
    dh = NSA_HEAD_DIM

    def nrm(k, shape, scale):
        return jax.random.normal(k, shape, jnp.float32) * scale

    return {
        "x": nrm(ks[0], (BATCH, SEQ, D_MODEL), 1.0),
        "norm_w": 1.0 + nrm(ks[1], (DEPTH, 2, D_MODEL), 0.02),
        "final_norm_w": 1.0 + nrm(ks[2], (D_MODEL,), 0.02),
        "ev_w_in": nrm(ks[3], (N_EVEN, D_MODEL, EVEN_IN), D_MODEL ** -0.5),
        "ev_conv_w": nrm(ks[4], (N_EVEN, CONV_WIDTH, 1, CONV_CH), CONV_WIDTH ** -0.5),
        "ev_conv_b": nrm(ks[5], (N_EVEN, CONV_CH), 0.02),
        "ev_conv_ln_w": 1.0 + nrm(ks[6], (N_EVEN, CONV_CH), 0.02),
        "ev_conv_ln_b": nrm(ks[7], (N_EVEN, CONV_CH), 0.02),
        "ev_cmp_pos": nrm(ks[8], (N_EVEN, 2, CMP_BLOCK, dh), 0.1),
        "ev_cmp_w1": nrm(ks[9], (N_EVEN, 2, CMP_BLOCK * dh, CMP_HIDDEN), (CMP_BLOCK * dh) ** -0.5),
        "ev_cmp_b1": nrm(ks[10], (N_EVEN, 2, CMP_HIDDEN), 0.02),
        "ev_cmp_w2": nrm(ks[11], (N_EVEN, 2, CMP_HIDDEN, dh), CMP_HIDDEN ** -0.5),
        "ev_cmp_b2": nrm(ks[12], (N_EVEN, 2, dh), 0.02),
        "ev_w_out": nrm(ks[13], (N_EVEN, MIX_WIDTH, D_MODEL), MIX_WIDTH ** -0.5),
        "od_w_in": nrm(ks[14], (N_ODD, D_MODEL, ODD_IN), D_MODEL ** -0.5),
        "od_lb_gamma": nrm(ks[15], (DEPTH, HG_HEADS * HG_DK), 0.5),
        "od_gnorm_w": 1.0 + nrm(ks[16], (N_ODD, HG_HEADS * HG_DV), 0.02),
        "od_w_out": nrm(ks[17], (N_ODD, HG_HEADS * HG_DV, D_MODEL), (HG_HEADS * HG_DV) ** -0.5),
        "ffn_w_gu": nrm(ks[18], (DEPTH, D_MODEL, 2 * FFN_HIDDEN), D_MODEL ** -0.5),
        "ffn_w_down": nrm(ks[19], (DEPTH, FFN_HIDDEN, D_MODEL), FFN_HIDDEN ** -0.5),
    }


def reference(x, norm_w, final_norm_w, ev_w_in, ev_conv_w, ev_conv_b, ev_conv_ln_w, ev_conv_ln_b,
              ev_cmp_pos, ev_cmp_w1, ev_cmp_b1, ev_cmp_w2, ev_cmp_b2, ev_w_out,
              od_w_in, od_lb_gamma, od_gnorm_w, od_w_out, ffn_w_gu, ffn_w_down):
    lb_all = jnp.cumsum(jax.nn.softmax(od_lb_gamma.astype(jnp.float32), axis=0), axis=0)
    lb_all = lb_all - lb_all[0]
    for layer in range(DEPTH):
        h = rmsnorm(x, norm_w[layer, 0])
        if layer % 2 == 0:
            e = layer // 2
            x = x + even_mixer(h, ev_w_in[e], ev_conv_w[e], ev_conv_b[e], ev_conv_ln_w[e], ev_conv_ln_b[e],
                               ev_cmp_pos[e], ev_cmp_w1[e], ev_cmp_b1[e], ev_cmp_w2[e], ev_cmp_b2[e], ev_w_out[e])
        else:
            o = layer // 2
            x = x + odd_mixer(h, od_w_in[o], lb_all[layer], od_gnorm_w[o], od_w_out[o])
        h = rmsnorm(x, norm_w[layer, 1])
        x = x + swiglu(h, ffn_w_gu[layer], ffn_w_down[layer])
    return rmsnorm(x, final_norm_w)
```

```python
from contextlib import ExitStack

import numpy as np
import ml_dtypes
import concourse.bass as bass
import concourse.mybir as mybir
from concourse.bass_utils import run_bass_kernel_spmd

F32 = mybir.dt.float32
BF16 = mybir.dt.bfloat16
AF = mybir.ActivationFunctionType
ALU = mybir.AluOpType
AX = mybir.AxisListType
NPBF = ml_dtypes.bfloat16

N_CORES = 8
PERIOD = 30000


class Sched:
    ENGS = ("pe", "act", "dve", "pool", "sp")

    def __init__(self, nc, ctx):
        self.nc = nc
        self.ctx = ctx
        self.q = {e: [] for e in self.ENGS}
        self.cnt = {e: 0 for e in self.ENGS}
        self.seen = {e: {} for e in self.ENGS}
        self.sems = {}
        self.last_w = {}
        self.readers = {}
        self.dma_cnt = {}
        self.nbuf = 0

    def sem(self, key):
        if key not in self.sems:
            name = "s_" + "_".join(str(k) for k in key)
            self.sems[key] = self.ctx.enter_context(self.nc.semaphore(name))
        return self.sems[key]

    def sb(self, name, shape, dt):
        return self.ctx.enter_context(self.nc.sbuf_tensor(name, list(shape), dt))

    def ps(self, name, shape, dt=F32):
        return self.ctx.enter_context(self.nc.psum_tensor(name, list(shape), dt))

    def _deps(self, eng, reads, writes, is_dma):
        deps = []
        for b in reads:
            ev = self.last_w.get(b)
            if ev is not None:
                deps.append((ev, True))
        for b in writes:
            ev = self.last_w.get(b)
            if ev is not None:
                deps.append((ev, False))
            for ev in self.readers.get(b, ()):
                deps.append((ev, False))
        waits = []
        for (ev, raw) in deps:
            key, val = ev
            if key[0] == "dma":
                val = self.dma_cnt[key] * 16
            elif key[1] == eng and not is_dma and not raw:
                continue
            if self.seen[eng].get(key, 0) < val:
                self.seen[eng][key] = val
                waits.append((key, val))
        best = {}
        for k, v in waits:
            best[k] = max(best.get(k, 0), v)
        return list(best.items())

    def _commit(self, ev, reads, writes):
        for b in reads:
            self.readers.setdefault(b, []).append(ev)
        for b in writes:
            self.last_w[b] = ev
            self.readers[b] = []

    def op(self, eng, fn, reads=(), writes=()):
        waits = self._deps(eng, reads, writes, False)
        n = self.cnt[eng]
        self.cnt[eng] = n + 1
        key = ("c", eng, n // PERIOD)
        val = (n % PERIOD) + 1
        self.q[eng].append((waits, fn, key, 1))
        self._commit((key, val), reads, writes)

    def dma(self, eng, out, in_, reads=(), writes=(), group=None, **kw):
        waits = self._deps(eng, reads, writes, True)
        if group is None:
            group = writes[0] if writes else reads[0]
        key = ("dma", group)
        self.dma_cnt[key] = self.dma_cnt.get(key, 0) + 1
        val = self.dma_cnt[key] * 16
        self.q[eng].append((waits, (lambda e, out=out, in_=in_, kw=kw: e.dma_start(out=out, in_=in_, **kw)), key, 16))
        self._commit((key, val), reads, writes)

    def emit(self):
        fin = []
        for key, c in self.dma_cnt.items():
            if self.seen["sp"].get(key, 0) < c * 16:
                fin.append((key, c * 16))
        for key in list(self.sems.keys()) + [k for k in self.dma_cnt]:
            self.sem(key)
        for e in self.ENGS:
            for waits, fn, key, inc in self.q[e]:
                self.sem(key)
                for k, v in waits:
                    self.sem(k)
        sems = self.sems
        q = self.q

        def run(e):
            def body(eng):
                for waits, fn, key, inc in q[e]:
                    for k, v in waits:
                        eng.wait_ge(sems[k], v)
                    fn(eng).then_inc(sems[key], inc)
                if e == "sp":
                    for k, v in fin:
                        eng.wait_ge(sems[k], v)
            return body

        with self.nc.Block() as block:
            block.tensor(run("pe"))
            block.scalar(run("act"))
            block.vector(run("dve"))
            block.gpsimd(run("pool"))
            block.sync(run("sp"))


def new_nc():
    return bass.Bass("TRN2", target_bir_lowering=False)


def build_convert(K_rows, N, with_scale):
    nc = new_nc()
    w = nc.dram_tensor("w", [K_rows, N], F32, kind="ExternalInput").ap()
    if with_scale:
        scl = nc.dram_tensor("sc", [K_rows], F32, kind="ExternalInput").ap()
    wb = nc.dram_tensor("wb", [K_rows, N], BF16, kind="ExternalOutput").ap()
    nch = K_rows // 128
    CW = max(d for d in range(1, 4097) if N % d == 0)
    with ExitStack() as ctx:
        S = Sched(nc, ctx)
        stg = [S.sb(f"stg{i}", [128, CW], F32) for i in range(2)]
        ob = [S.sb(f"ob{i}", [128, CW], BF16) for i in range(2)]
        if with_scale:
            sct = S.sb("sct", [128, nch], F32)
            S.dma("sp", sct[:], scl.rearrange("(c p) -> p c", p=128), writes=["sct"], allow_slow_non_contiguous=True)
        it = 0
        for c in range(nch):
            for n0 in range(0, N, CW):
                i = it % 2
                it += 1
                S.dma("sp", stg[i][:], w[c * 128:(c + 1) * 128, n0:n0 + CW], writes=[("stg", i)])
                if with_scale:
                    S.op("dve", lambda e, i=i, c=c: e.tensor_scalar(ob[i][:], stg[i][:], sct[:, c:c + 1], None, ALU.mult),
                         reads=[("stg", i), "sct"], writes=[("ob", i)])
                else:
                    S.op("dve", lambda e, i=i: e.tensor_copy(ob[i][:], stg[i][:]), reads=[("stg", i)], writes=[("ob", i)])
                S.dma("pool", wb[c * 128:(c + 1) * 128, n0:n0 + CW], ob[i][:], reads=[("ob", i)], group=("st", i))
        S.emit()
    return nc


EPS = 1e-6
TT = 512
NS = TT // 128
D = 2048
NCH = D // 128
FFN_H = 5632
HCH = FFN_H // 128


class TPBase:
    def __init__(self, nc, ctx, nw=3):
        self.nc = nc
        self.S = Sched(nc, ctx)
        S = self.S
        self.bank = [S.ps(f"bank{i}", [128, 512], F32) for i in range(8)]
        self.wbuf = [S.sb(f"wbuf{i}", [128, 16, 512], BF16) for i in range(nw)]
        self.nw = nw
        self.wi = 0
        self.ident = S.sb("ident_sb", [128, 128], BF16)
        self.dq = 0

    def load_ident(self, ident_dram):
        self.S.dma("sp", self.ident[:], ident_dram, writes=["ident"])

    def wload(self, w_dram, r0, nch, n0, ncols=512):
        i = self.wi % self.nw
        self.wi += 1
        eng = "sp" if (self.dq % 2 == 0) else "pool"
        self.dq += 1
        src = w_dram[r0:r0 + nch * 128, n0:n0 + ncols].rearrange("(c p) n -> p c n", p=128)
        self.S.dma(eng, self.wbuf[i][:, 0:nch, 0:ncols], src, writes=[("w", i)])
        return i

    def rmsnorm_T(self, xt, hn, hT, ss, rs, tag, nsub=NS, tcol0=0):
        S = self.S
        for s in range(nsub):
            S.op("act", lambda e, s=s: e.activation(out=hn[:, s, :], in_=xt[:, s, :], func=AF.Square,
                                                    accum_out=ss[:, s:s + 1]),
                 reads=[("xt", s)], writes=[("hn", s), ("ss", s)])
        S.op("dve", lambda e: e.tensor_scalar(rs[:, 0:nsub], ss[:, 0:nsub], 1.0 / D, EPS, ALU.mult, ALU.add),
             reads=[("ss", s) for s in range(nsub)], writes=["rs"])
        S.op("act", lambda e: e.activation(out=rs[:, 0:nsub], in_=rs[:, 0:nsub], func=AF.Sqrt), reads=["rs"], writes=["rs"])
        S.op("dve", lambda e: e.reciprocal(rs[:, 0:nsub], rs[:, 0:nsub]), reads=["rs"], writes=["rs"])
        for s in range(nsub):
            S.op("act", lambda e, s=s: e.activation(out=hn[:, s, :], in_=xt[:, s, :], func=AF.Copy, scale=rs[:, s:s + 1]),
                 reads=[("xt", s), "rs"], writes=[("hn", s)])
            for half in range(2):
                b = 4 + ((2 * s + half) % 4)
                pT = self.bank[b][:].bitcast(BF16)
                for k in range(8):
                    c = half * 8 + k
                    S.op("pe", lambda e, pT=pT, k=k, c=c, s=s: e.transpose(pT[:, k * 128:(k + 1) * 128],
                                                                         hn[:, s, c * 128:(c + 1) * 128], self.ident[:]),
                         reads=[("hn", s), "ident"], writes=[("bank", b)])
                eng = "dve" if half == 0 else "act"
                dst = hT[:, half * 8:(half + 1) * 8, tcol0 + s * 128:tcol0 + (s + 1) * 128]
                src = pT.rearrange("p (k t) -> p k t", k=8)
                if eng == "dve":
                    S.op("dve", lambda e, dst=dst, src=src: e.tensor_copy(dst, src), reads=[("bank", b)], writes=[("hT", tag)])
                else:
                    S.op("act", lambda e, dst=dst, src=src: e.activation(out=dst, in_=src, func=AF.Copy),
                         reads=[("bank", b)], writes=[("hT", tag)])

    def proj_fm(self, w_dram, n0, hT, bank_ids, epilogue, kch=NCH, ntok=TT, hT_tag=0):
        S = self.S
        wi = self.wload(w_dram, 0, kch, n0)
        wb = self.wbuf[wi]
        for jj in range(4):
            b = bank_ids[jj % len(bank_ids)]
            for c in range(kch):
                S.op("pe", lambda e, b=b, c=c, jj=jj, wb=wb: e.matmul(self.bank[b][:, 0:ntok], wb[:, c, jj * 128:(jj + 1) * 128],
                                                                   hT[:, c, 0:ntok], start=(c == 0), stop=(c == kch - 1)),
                     reads=[("w", wi), ("hT", hT_tag)], writes=[("bank", b)])
            epilogue(jj, self.bank[b][:, 0:ntok], ("bank", b))

    def proj_tm(self, w_dram, r0, kch, n0, actT, act_tag, first, last, nsub=NS, ncols=512):
        S = self.S
        wi = self.wload(w_dram, r0, kch, n0, ncols)
        wb = self.wbuf[wi]
        for s in range(nsub):
            for c in range(kch):
                S.op("pe", lambda e, s=s, c=c, wb=wb: e.matmul(self.bank[s][:, 0:ncols], actT(c, s), wb[:, c, 0:ncols],
                                                             start=(first and c == 0), stop=(last and c == kch - 1)),
                     reads=[("w", wi), act_tag], writes=[("bank", s)])


def build_stageB(Tc, mode):
    fin = mode == "fin"
    nc = new_nc()
    dt = nc.dram_tensor
    x = dt("x", [Tc, D], F32, kind="ExternalInput").ap()
    mixT = dt("mixT", [D, Tc], BF16, kind="ExternalInput").ap()
    wo = dt("wo", [D, D], BF16, kind="ExternalInput").ap()
    wgu = dt("wgu", [D, 2 * FFN_H], BF16, kind="ExternalInput").ap()
    wd = dt("wd", [FFN_H, D], BF16, kind="ExternalInput").ap()
    identd = dt("ident", [128, 128], BF16, kind="ExternalInput").ap()
    if fin:
        sgT = dt("sgT", [D, Tc], BF16, kind="ExternalInput").ap()
        fnw = dt("fnw", [D], F32, kind="ExternalInput").ap()
        out = dt("out", [Tc, D], F32, kind="ExternalOutput").ap()
    else:
        win2 = dt("win2", [D, 4 * D], BF16, kind="ExternalInput").ap()
        gam = dt("gam", [2, D], F32, kind="ExternalInput").ap()
        x2 = dt("x2", [Tc, D], F32, kind="ExternalOutput").ap()
        qT_o = dt("qT", [D, Tc], BF16, kind="ExternalOutput").ap()
        kT_o = dt("kT", [D, Tc], BF16, kind="ExternalOutput").ap()
        lfT_o = dt("lfT", [D, Tc], F32, kind="ExternalOutput").ap()
        iv_o = dt("iv", [Tc, D], BF16, kind="ExternalOutput").ap()
        sgT_o = dt("sgTo", [D, Tc], BF16, kind="ExternalOutput").ap()

    with ExitStack() as ctx:
        P = TPBase(nc, ctx, nw=4)
        S = P.S
        P.load_ident(identd)
        xt = S.sb("xt", [128, NS, D], F32)
        hn = S.sb("hn", [128, NS, D], BF16)
        hT = S.sb("hT", [128, NCH, TT], BF16)
        hid = S.sb("hid", [128, HCH, TT], BF16)
        tmpA = [S.sb(f"tmpA{i}", [128, TT], F32) for i in range(2)]
        ss = S.sb("ss", [128, NS], F32)
        rs = S.sb("rs", [128, NS], F32)
        if fin:
            fwb = S.sb("fwb", [128, D], F32)
            S.dma("sp", fwb[:], fnw.partition_broadcast(128), writes=["fwb"])
        else:
            gt = S.sb("gt", [128, 2, NCH], F32)
            lb = S.sb("lb", [128, NCH], F32)
            oml = S.sb("oml", [128, NCH], F32)
            S.dma("sp", gt[:], gam.rearrange("l (c p) -> p l c", p=128), writes=["gt"], allow_slow_non_contiguous=True)
            S.op("dve", lambda e: e.tensor_sub(lb[:], gt[:, 1, :], gt[:, 0, :]), reads=["gt"], writes=["lb"])
            S.op("act", lambda e: e.activation(out=lb[:], in_=lb[:], func=AF.Sigmoid), reads=["lb"], writes=["lb"])
            S.op("dve", lambda e: e.tensor_scalar(oml[:], lb[:], -1.0, 1.0, ALU.mult, ALU.add), reads=["lb"], writes=["oml"])
            ostg = [S.sb(f"ostg{i}", [128, 4, TT], BF16) for i in range(2)]
            ostf = [S.sb(f"ostf{i}", [128, 4, TT], F32) for i in range(2)]
            ftmp = [S.sb(f"ftmp{i}", [128, TT], F32) for i in range(2)]

        XT = [("xt", s) for s in range(NS)]
        for t0 in range(0, Tc, TT):
            mt = hid
            S.dma("pool", mt[:, 0:NCH, :], mixT[:, t0:t0 + TT].rearrange("(c p) t -> p c t", p=128), writes=["hid"])
            if fin:
                S.dma("pool", hid[:, NCH:2 * NCH, :], sgT[:, t0:t0 + TT].rearrange("(c p) t -> p c t", p=128), writes=["hid2"])
                S.op("dve", lambda e: e.tensor_mul(hid[:, 0:NCH, :], hid[:, 0:NCH, :], hid[:, NCH:2 * NCH, :]),
                     reads=["hid", "hid2"], writes=["hid"])
            S.dma("sp", xt[:], x[t0:t0 + TT, :].rearrange("(s p) d -> p s d", p=128), writes=XT)
            for nb in range(4):
                P.proj_tm(wo, 0, NCH, nb * 512, lambda c, s: mt[:, c, s * 128:(s + 1) * 128], "hid", True, True)
                for s in range(NS):
                    S.op("dve", lambda e, s=s, nb=nb: e.tensor_tensor(xt[:, s, nb * 512:(nb + 1) * 512], P.bank[s][:],
                                                                      xt[:, s, nb * 512:(nb + 1) * 512], ALU.add),
                         reads=[("bank", s), ("xt", s)], writes=[("xt", s)])
            P.rmsnorm_T(xt, hn, hT, ss, rs, 0)
            for jb in range(HCH // 4):
                wa = P.wload(wgu, 0, NCH, jb * 512)
                wbi = P.wload(wgu, 0, NCH, FFN_H + jb * 512)
                for jj in range(4):
                    j = jb * 4 + jj
                    i = j % 2
                    bA, bB = (4, 5) if i == 0 else (6, 7)
                    for (bb, wi_) in ((bA, wa), (bB, wbi)):
                        for c in range(NCH):
                            S.op("pe", lambda e, bb=bb, wi_=wi_, c=c, jj=jj: e.matmul(
                                P.bank[bb][:], P.wbuf[wi_][:, c, jj * 128:(jj + 1) * 128], hT[:, c, :],
                                start=(c == 0), stop=(c == NCH - 1)),
                                reads=[("w", wi_), ("hT", 0)], writes=[("bank", bb)])
                    S.op("act", lambda e, i=i, bA=bA: e.activation(out=tmpA[i][:], in_=P.bank[bA][:], func=AF.Silu),
                         reads=[("bank", bA)], writes=[("tmpA", i)])
                    S.op("dve", lambda e, i=i, j=j, bB=bB: e.tensor_tensor(hid[:, j, :], tmpA[i][:], P.bank[bB][:], ALU.mult),
                         reads=[("bank", bB), ("tmpA", i)], writes=["hid"])
            kbs = [(0, 16), (16, 16), (32, HCH - 32)]
            for nb in range(4):
                for ki, (k0, kn) in enumerate(kbs):
                    P.proj_tm(wd, k0 * 128, kn, nb * 512, lambda c, s, k0=k0: hid[:, k0 + c, s * 128:(s + 1) * 128], "hid",
                              ki == 0, ki == len(kbs) - 1)
                for s in range(NS):
                    S.op("dve", lambda e, s=s, nb=nb: e.tensor_tensor(xt[:, s, nb * 512:(nb + 1) * 512], P.bank[s][:],
                                                                      xt[:, s, nb * 512:(nb + 1) * 512], ALU.add),
                         reads=[("bank", s), ("xt", s)], writes=[("xt", s)])
            if fin:
                for s in range(NS):
                    S.op("act", lambda e, s=s: e.activation(out=hn[:, s, :], in_=xt[:, s, :], func=AF.Square,
                                                            accum_out=ss[:, s:s + 1]),
                         reads=[("xt", s)], writes=[("hn", s), ("ss", s)])
                S.op("dve", lambda e: e.tensor_scalar(rs[:], ss[:], 1.0 / D, EPS, ALU.mult, ALU.add),
                     reads=[("ss", s) for s in range(NS)], writes=["rs"])
                S.op("act", lambda e: e.activation(out=rs[:], in_=rs[:], func=AF.Sqrt), reads=["rs"], writes=["rs"])
                S.op("dve", lambda e: e.reciprocal(rs[:], rs[:]), reads=["rs"], writes=["rs"])
                for s in range(NS):
                    S.op("dve", lambda e, s=s: e.scalar_tensor_tensor(xt[:, s, :], xt[:, s, :], rs[:, s:s + 1], fwb[:],
                                                                      ALU.mult, ALU.mult),
                         reads=[("xt", s), "rs", "fwb"], writes=[("xt", s)])
                S.dma("pool", out[t0:t0 + TT, :].rearrange("(s p) d -> p s d", p=128), xt[:], reads=XT, group="st_out")
            else:
                S.dma("pool", x2[t0:t0 + TT, :].rearrange("(s p) d -> p s d", p=128), xt[:], reads=XT, group="st_x2")
                P.rmsnorm_T(xt, hn, hT, ss, rs, 0)
                oc = [0]

                def fm_group(col0, dst_dram, kind):
                    for blk in range(4):
                        oi = oc[0] % 2
                        oc[0] += 1

                        def epi(jj, ps, tok, blk=blk, oi=oi):
                            cch = blk * 4 + jj
                            if kind == "q":
                                S.op("act", lambda e: e.activation(out=ostg[oi][:, jj, :], in_=ps, func=AF.Copy),
                                     reads=[tok], writes=[("ostg", oi)])
                            elif kind == "g":
                                S.op("act", lambda e: e.activation(out=ostg[oi][:, jj, :], in_=ps, func=AF.Silu),
                                     reads=[tok], writes=[("ostg", oi)])
                            else:
                                fi = (blk * 4 + jj) % 2
                                S.op("act", lambda e: e.activation(out=ftmp[fi][:], in_=ps, func=AF.Sigmoid),
                                     reads=[tok], writes=[("ftmp", fi)])
                                S.op("dve", lambda e: e.tensor_scalar(ftmp[fi][:], ftmp[fi][:], oml[:, cch:cch + 1],
                                                                      lb[:, cch:cch + 1], ALU.mult, ALU.add),
                                     reads=[("ftmp", fi), "oml", "lb"], writes=[("ftmp", fi)])
                                S.op("dve", lambda e: e.tensor_scalar(ostg[oi][:, jj, :], ftmp[fi][:], -1.0, 1.0, ALU.mult, ALU.add),
                                     reads=[("ftmp", fi)], writes=[("ostg", oi)])
                                S.op("act", lambda e: e.activation(out=ostf[oi][:, jj, :], in_=ftmp[fi][:], func=AF.Ln),
                                     reads=[("ftmp", fi)], writes=[("ostf", oi)])
                        P.proj_fm(win2, col0 + blk * 512, hT, [4, 5, 6, 7], epi)
                        rows = slice(blk * 512, (blk + 1) * 512)
                        if kind == "f":
                            S.dma("pool", kT_o[rows, t0:t0 + TT].rearrange("(c p) t -> p c t", p=128), ostg[oi][:],
                                  reads=[("ostg", oi)], group=("sto", oi))
                            S.dma("pool", lfT_o[rows, t0:t0 + TT].rearrange("(c p) t -> p c t", p=128), ostf[oi][:],
                                  reads=[("ostf", oi)], group=("stf", oi))
                        else:
                            S.dma("pool", dst_dram[rows, t0:t0 + TT].rearrange("(c p) t -> p c t", p=128), ostg[oi][:],
                                  reads=[("ostg", oi)], group=("sto", oi))
                fm_group(0, qT_o, "q")
                fm_group(D, None, "f")
                fm_group(3 * D, sgT_o, "g")
                for nb in range(4):
                    P.proj_tm(win2, 0, NCH, 2 * D + nb * 512, lambda c, s: hT[:, c, s * 128:(s + 1) * 128], ("hT", 0), True, True)
                    for s in range(NS):
                        S.op("act", lambda e, s=s, nb=nb: e.activation(out=hn[:, s, nb * 512:(nb + 1) * 512], in_=P.bank[s][:],
                                                                       func=AF.Copy),
                             reads=[("bank", s)], writes=[("hn", s)])
                S.dma("pool", iv_o[t0:t0 + TT, :].rearrange("(s p) d -> p s d", p=128), hn[:],
                      reads=[("hn", s) for s in range(NS)], group="st_iv")
        S.emit()
    return nc


HG_C = 64


def build_hgrn(Sq, HPC=4, TL=512):
    nc = new_nc()
    dt = nc.dram_tensor
    R = HPC * 128
    qT = dt("qT", [R, Sq], BF16, kind="ExternalInput").ap()
    kT = dt("kT", [R, Sq], BF16, kind="ExternalInput").ap()
    lfT = dt("lfT", [R, Sq], F32, kind="ExternalInput").ap()
    iv = dt("iv", [Sq, R], BF16, kind="ExternalInput").ap()
    gnw = dt("gnw", [R], F32, kind="ExternalInput").ap()
    identd = dt("ident", [128, 128], BF16, kind="ExternalInput").ap()
    maskd = dt("maskT", [64, 64], F32, kind="ExternalInput").ap()
    rmaskd = dt("rmask", [128, TL], F32, kind="ExternalInput").ap()
    oT = dt("oT", [R, Sq], BF16, kind="ExternalOutput").ap()
    NCk = TL // HG_C
    with ExitStack() as ctx:
        S = Sched(nc, ctx)
        bank = [S.ps(f"bank{i}", [128, 512], F32) for i in range(8)]
        ident = S.sb("ident_sb", [128, 128], BF16)
        ones = S.sb("ones_sb", [128, 128], BF16)
        maskT = S.sb("maskT_sb", [64, 64], F32)
        rmask = S.sb("rmask_sb", [128, TL], F32)
        gn = S.sb("gn_sb", [128, HPC], F32)
        S.dma("sp", ident[:], identd, writes=["ident"])
        S.dma("sp", maskT[:], maskd, writes=["maskT"])
        S.dma("sp", rmask[:], rmaskd, writes=["rmask"])
        S.dma("sp", gn[:], gnw.rearrange("(h p) -> p h", p=128), writes=["gn"], allow_slow_non_contiguous=True)
        S.op("pool", lambda e: e.memset(ones[:], 1.0), writes=["ones"])
        NSET = 2 * HPC
        mk = lambda nm, shp, d: [S.sb(f"{nm}{i}", shp, d) for i in range(NSET)]
        qt = mk("qt", [128, TL], BF16)
        kt = mk("kt", [128, TL], BF16)
        lf = mk("lf", [128, TL], F32)
        Gc = mk("Gc", [128, TL], F32)
        E1 = mk("E1", [128, TL], F32)
        E2 = mk("E2", [128, TL], F32)
        qd = mk("qd", [128, TL], BF16)
        kd = mk("kd", [128, TL], BF16)
        sm = mk("sm", [128, 4, NCk], F32)
        ivt = [S.sb(f"ivt{i}", [64, NCk, R], BF16) for i in range(2)]
        Sst = [S.sb(f"Sst{h}", [128, 128], F32) for h in range(HPC)]
        SsM = [S.sb(f"SsM{h}", [128, 128], BF16) for h in range(HPC)]
        kdT = [S.sb(f"kdT{h}", [64, 128], BF16) for h in range(HPC)]
        AT = [S.sb(f"AT{h}", [64, 64], BF16) for h in range(HPC)]
        osb = [S.sb(f"osb{h}", [128, TL], F32) for h in range(HPC)]
        osq = [S.sb(f"osq{h}", [128, TL], BF16) for h in range(HPC)]
        rstd = [S.sb(f"rstd{h}", [128, TL], F32) for h in range(HPC)]
        ob = [S.sb(f"obf{h}", [128, TL], BF16) for h in range(HPC)]
        for h in range(HPC):
            S.op("pool", lambda e, h=h: e.memset(Sst[h][:], 0.0), writes=[("Sst", h)])

        ntile = Sq // TL
        for ti in range(ntile):
            t0 = ti * TL
            ip = ti % 2
            S.dma("pool", ivt[ip][:], iv[t0:t0 + TL, :].rearrange("(n s) v -> s n v", s=HG_C), writes=[("ivt", ip)])
            for h in range(HPC):
                p = ip * HPC + h
                rows = slice(h * 128, (h + 1) * 128)
                S.dma("sp", qt[p][:], qT[rows, t0:t0 + TL], writes=[("qt", p)])
                S.dma("sp", kt[p][:], kT[rows, t0:t0 + TL], writes=[("kt", p)])
                S.dma("sp", lf[p][:], lfT[rows, t0:t0 + TL], writes=[("lf", p)])
                S.op("dve", lambda e, p=p: e.tensor_tensor_scan(Gc[p][:], rmask[:], lf[p][:], 0.0, ALU.mult, ALU.add),
                     reads=[("lf", p), "rmask"], writes=[("Gc", p)])
                G3 = Gc[p][:].rearrange("p (n c) -> p n c", c=HG_C)
                S.op("dve", lambda e, p=p, G3=G3: e.tensor_scalar(sm[p][:, 0, :], G3[:, :, HG_C // 2 - 1], -1.0, None, ALU.mult),
                     reads=[("Gc", p)], writes=[("sm0", p)])
                S.op("dve", lambda e, p=p, G3=G3: e.tensor_tensor(sm[p][:, 2, :], G3[:, :, HG_C - 1], sm[p][:, 0, :], ALU.add),
                     reads=[("Gc", p), ("sm0", p)], writes=[("sm2", p)])
                S.op("act", lambda e, p=p, G3=G3: e.activation(out=sm[p][:, 3, :], in_=G3[:, :, HG_C - 1], func=AF.Exp),
                     reads=[("Gc", p)], writes=[("sm3", p)])
                S.op("act", lambda e, p=p: e.activation(out=sm[p][:, 1, :], in_=sm[p][:, 0, :], func=AF.Exp, scale=-1.0),
                     reads=[("sm0", p)], writes=[("sm1", p)])
                S.op("act", lambda e, p=p: e.activation(out=sm[p][:, 2, :], in_=sm[p][:, 2, :], func=AF.Exp),
                     reads=[("sm2", p)], writes=[("sm2", p)])
                S.op("dve", lambda e, p=p, G3=G3: e.tensor_tensor(G3, G3, sm[p][:, 0, :].unsqueeze(2).to_broadcast([128, NCk, HG_C]), ALU.add),
                     reads=[("Gc", p), ("sm0", p), ("sm3", p), ("sm2", p)], writes=[("Gc", p)])
                S.op("act", lambda e, p=p: e.activation(out=E1[p][:], in_=Gc[p][:], func=AF.Exp), reads=[("Gc", p)], writes=[("E1", p)])
                S.op("act", lambda e, p=p: e.activation(out=E2[p][:], in_=Gc[p][:], func=AF.Exp, scale=-1.0),
                     reads=[("Gc", p)], writes=[("E2", p)])
                S.op("pool", lambda e, p=p: e.tensor_tensor(qd[p][:], qt[p][:], E1[p][:], ALU.mult),
                     reads=[("qt", p), ("E1", p)], writes=[("qd", p)])
                S.op("pool", lambda e, p=p: e.tensor_tensor(kd[p][:], kt[p][:], E2[p][:], ALU.mult),
                     reads=[("kt", p), ("E2", p)], writes=[("kd", p)])
            for n in range(NCk):
                cs = slice(n * HG_C, (n + 1) * HG_C)
                for h in range(HPC):
                    p = ip * HPC + h
                    bk = bank[4 + h]
                    psA = bk[0:64, 0:64]
                    psT = bk[:].bitcast(BF16)[0:64, 512:640]
                    S.op("pe", lambda e, psT=psT, p=p, cs=cs: e.transpose(psT, kd[p][:, cs], ident[:]),
                         reads=[("kd", p), "ident"], writes=[("bk", h)])
                    S.op("pe", lambda e, psA=psA, p=p, cs=cs: e.matmul(psA, kd[p][:, cs], qd[p][:, cs], start=True, stop=True),
                         reads=[("kd", p), ("qd", p)], writes=[("bk", h)])
                    S.op("dve", lambda e, psT=psT, h=h: e.tensor_copy(kdT[h][:], psT),
                         reads=[("bk", h)], writes=[("kdT", h)])
                    S.op("dve", lambda e, psA=psA, h=h: e.tensor_tensor(AT[h][:], psA, maskT[:], ALU.mult),
                         reads=[("bk", h), "maskT"], writes=[("AT", h)])
                    S.op("dve", lambda e, h=h, p=p, n=n: e.tensor_scalar(SsM[h][:], Sst[h][:], sm[p][:, 1, n:n + 1], None, ALU.mult),
                         reads=[("Sst", h), ("sm1", p)], writes=[("SsM", h)])
                for h in range(HPC):
                    p = ip * HPC + h
                    bk = bank[4 + h]
                    psU = bk[:, 128:256]
                    pso = bank[h]
                    iv_c = ivt[ip][:, n, h * 128:(h + 1) * 128]
                    S.op("pe", lambda e, pso=pso, iv_c=iv_c, h=h, cs=cs: e.matmul(pso[:, cs], iv_c, AT[h][:], start=True, stop=False),
                         reads=[("ivt", ip), ("AT", h)], writes=[("pso", h)])
                    S.op("pe", lambda e, pso=pso, h=h, p=p, cs=cs: e.matmul(pso[:, cs], SsM[h][:], qd[p][:, cs], start=False, stop=True),
                         reads=[("SsM", h), ("qd", p)], writes=[("pso", h)])
                    S.op("pe", lambda e, psU=psU, h=h, iv_c=iv_c: e.matmul(psU, kdT[h][:], iv_c, start=True, stop=True),
                         reads=[("kdT", h), ("ivt", ip)], writes=[("bk", h)])
                    S.op("dve", lambda e, h=h, p=p, n=n: e.tensor_scalar(Sst[h][:], Sst[h][:], sm[p][:, 3, n:n + 1], None, ALU.mult),
                         reads=[("Sst", h), ("sm3", p)], writes=[("Sst", h)])
                    S.op("dve", lambda e, h=h, p=p, n=n, psU=psU: e.scalar_tensor_tensor(Sst[h][:], psU, sm[p][:, 2, n:n + 1], Sst[h][:],
                                                                                         ALU.mult, ALU.add),
                         reads=[("bk", h), ("Sst", h), ("sm2", p)], writes=[("Sst", h)])
            for h in range(HPC):
                pso = bank[h]
                S.op("act", lambda e, h=h, pso=pso: e.activation(out=osb[h][:], in_=pso[:, 0:TL], func=AF.Copy),
                     reads=[("pso", h)], writes=[("osb", h)])
                S.op("act", lambda e, h=h: e.activation(out=osq[h][:], in_=osb[h][:], func=AF.Square),
                     reads=[("osb", h)], writes=[("osq", h)])
                S.op("pe", lambda e, h=h, pso=pso: e.matmul(pso[:, 0:TL], ones[:], osq[h][:], start=True, stop=True),
                     reads=["ones", ("osq", h)], writes=[("pso", h)])
                S.op("dve", lambda e, h=h, pso=pso: e.tensor_scalar(rstd[h][:], pso[:, 0:TL], 1.0 / 128, EPS, ALU.mult, ALU.add),
                     reads=[("pso", h)], writes=[("rstd", h)])
                S.op("act", lambda e, h=h: e.activation(out=rstd[h][:], in_=rstd[h][:], func=AF.Sqrt),
                     reads=[("rstd", h)], writes=[("rstd", h)])
                S.op("dve", lambda e, h=h: e.reciprocal(rstd[h][:], rstd[h][:]), reads=[("rstd", h)], writes=[("rstd", h)])
                S.op("dve", lambda e, h=h: e.scalar_tensor_tensor(ob[h][:], osb[h][:], gn[:, h:h + 1], rstd[h][:], ALU.mult, ALU.mult),
                     reads=[("osb", h), ("rstd", h), "gn"], writes=[("ob", h)])
                S.dma("pool", oT[h * 128:(h + 1) * 128, t0:t0 + TL], ob[h][:], reads=[("ob", h)], group=("sto", h))
        S.emit()
    return nc


CONV_CH = 1024
CONV_W = 31
EV_IN = 4656


def build_stageA(Tc):
    nc = new_nc()
    dt = nc.dram_tensor
    xh = dt("xh", [Tc + 32, D], F32, kind="ExternalInput").ap()
    win = dt("win", [D, EV_IN], BF16, kind="ExternalInput").ap()
    cwd = dt("conv_w", [CONV_W, CONV_CH], F32, kind="ExternalInput").ap()
    cbd = dt("conv_b", [CONV_CH], F32, kind="ExternalInput").ap()
    lwd = dt("ln_w", [CONV_CH], F32, kind="ExternalInput").ap()
    lbd = dt("ln_b", [CONV_CH], F32, kind="ExternalInput").ap()
    identd = dt("ident", [128, 128], BF16, kind="ExternalInput").ap()
    aT_o = dt("aT", [CONV_CH, Tc], BF16, kind="ExternalOutput").ap()
    qT_o = dt("qT", [1024, Tc], BF16, kind="ExternalOutput").ap()
    kcT_o = dt("kcT", [256, Tc], BF16, kind="ExternalOutput").ap()
    vcT_o = dt("vcT", [256, Tc], BF16, kind="ExternalOutput").ap()
    ksT_o = dt("ksT", [256, Tc], BF16, kind="ExternalOutput").ap()
    kwT_o = dt("kwT", [256, Tc], BF16, kind="ExternalOutput").ap()
    vs_o = dt("vs", [Tc, 256], BF16, kind="ExternalOutput").ap()
    vw_o = dt("vw", [Tc, 256], BF16, kind="ExternalOutput").ap()
    gt_o = dt("gates", [Tc, 48], F32, kind="ExternalOutput").ap()
    NCC = CONV_CH // 128
    GW = TT + 30
    with ExitStack() as ctx:
        P = TPBase(nc, ctx)
        S = P.S
        P.load_ident(identd)
        xt = S.sb("xt", [128, NS, D], F32)
        hn = S.sb("hn", [128, NS, D], BF16)
        hT = S.sb("hT", [128, NCH, TT], BF16)
        ss = S.sb("ss", [128, NS], F32)
        rs = S.sb("rs", [128, NS], F32)
        ones = S.sb("ones_sb", [128, 128], BF16)
        S.op("pool", lambda e: e.memset(ones[:], 1.0), writes=["ones"])
        cw = S.sb("cw", [128, NCC, CONV_W], F32)
        cb = S.sb("cb", [128, NCC], F32)
        lw = S.sb("lw", [128, NCC], F32)
        lbb = S.sb("lbb", [128, NCC], F32)
        for c_ in range(NCC):
            S.dma("sp", cw[:, c_, :], cwd[:, c_ * 128:(c_ + 1) * 128].rearrange("k p -> p k"), writes=["cw"],
                  allow_slow_non_contiguous=True)
        for (t_, d_, nm) in ((cb, cbd, "cb"), (lw, lwd, "lw"), (lbb, lbd, "lbb")):
            S.dma("sp", t_[:], d_.rearrange("(c p) -> p c", p=128), writes=[nm], allow_slow_non_contiguous=True)
        gl = [S.sb(f"gl{c}", [128, GW], F32) for c in range(NCC)]
        acc = [S.sb(f"acc{c}", [128, TT], F32) for c in range(NCC)]
        acc2 = [S.sb(f"accp{i}", [128, TT], F32) for i in range(2)]
        abf = [S.sb(f"abf{i}", [128, TT], BF16) for i in range(2)]
        asq = [S.sb(f"asq{i}", [128, TT], BF16) for i in range(2)]
        sgt = [S.sb(f"sgt{i}", [128, TT], F32) for i in range(2)]
        mean = S.sb("mean", [128, TT], F32)
        rstd = S.sb("rstd", [128, TT], F32)
        msq = S.sb("msq", [128, TT], F32)
        ostg = [S.sb(f"ostg{i}", [128, 4, TT], BF16) for i in range(2)]
        vst = [S.sb(f"vst{i}", [128, NS, 256], BF16) for i in range(2)]
        gst = S.sb("gst", [128, NS, 48], F32)
        oc = [0]

        def glu_blocks(ntok, dst_of):
            for ab in range(2):
                wa = P.wload(win, 0, NCH, ab * 512)
                wg = P.wload(win, 0, NCH, CONV_CH + ab * 512)
                for jj in range(4):
                    cc = ab * 4 + jj
                    i = cc % 2
                    bA, bB = (4, 5) if i == 0 else (6, 7)
                    for (bb, wi_) in ((bA, wa), (bB, wg)):
                        for c in range(NCH):
                            S.op("pe", lambda e, bb=bb, wi_=wi_, c=c, jj=jj: e.matmul(
                                P.bank[bb][:, 0:ntok], P.wbuf[wi_][:, c, jj * 128:(jj + 1) * 128], hT[:, c, 0:ntok],
                                start=(c == 0), stop=(c == NCH - 1)),
                                reads=[("w", wi_), ("hT", 0)], writes=[("bank", bb)])
                    dst, c0, n = dst_of(cc)
                    S.op("act", lambda e, i=i, bB=bB, c0=c0, n=n: e.activation(out=sgt[i][:, 0:n], in_=P.bank[bB][:, c0:c0 + n],
                                                                               func=AF.Sigmoid),
                         reads=[("bank", bB)], writes=[("sgt", i)])
                    S.op("dve", lambda e, i=i, bA=bA, dst=dst, c0=c0, n=n: e.tensor_tensor(dst, P.bank[bA][:, c0:c0 + n], sgt[i][:, 0:n],
                                                                                          ALU.mult),
                         reads=[("bank", bA), ("sgt", i)], writes=[("gl", cc)])

        S.op("pool", lambda e: e.memset(xt[:, 0, :], 0.0), writes=[("xt", 0)])
        S.dma("sp", xt[0:32, 0, :], xh[0:32, :], writes=[("xt", 0)])
        P.rmsnorm_T(xt, hn, hT, ss, rs, 0, nsub=1)
        glu_blocks(128, lambda cc: (gl[cc][:, 0:30], 2, 30))

        XT = [("xt", s) for s in range(NS)]
        for t0 in range(0, Tc, TT):
            S.dma("sp", xt[:], xh[32 + t0:32 + t0 + TT, :].rearrange("(s p) d -> p s d", p=128), writes=XT)
            P.rmsnorm_T(xt, hn, hT, ss, rs, 0)
            glu_blocks(TT, lambda cc: (gl[cc][:, 30:GW], 0, TT))
            fm_jobs = [(2048, [0, 1, 2, 3], [(qT_o, 0)] * 4, 0.125), (2560, [0, 1, 2, 3], [(qT_o, 512)] * 4, 0.125),
                       (3072, [0, 1, 2, 3], [(kcT_o, 0), (kcT_o, 0), (vcT_o, -256), (vcT_o, -256)], 1.0),
                       (3584, [0, 1], [(ksT_o, 0), (ksT_o, 0)], 1.0), (4096, [0, 1], [(kwT_o, 0), (kwT_o, 0)], 1.0)]
            for (n0, chunks, dsts, scl) in fm_jobs:
                wi = P.wload(win, 0, NCH, n0)
                oi = oc[0] % 2
                oc[0] += 1
                for jj in chunks:
                    b = 4 + jj
                    for c in range(NCH):
                        S.op("pe", lambda e, b=b, c=c, jj=jj, wi=wi: e.matmul(P.bank[b][:], P.wbuf[wi][:, c, jj * 128:(jj + 1) * 128],
                                                                              hT[:, c, :], start=(c == 0), stop=(c == NCH - 1)),
                             reads=[("w", wi), ("hT", 0)], writes=[("bank", b)])
                    S.op("act", lambda e, b=b, jj=jj, oi=oi, scl=scl: e.activation(out=ostg[oi][:, jj, :], in_=P.bank[b][:], func=AF.Copy,
                                                                                   scale=scl),
                         reads=[("bank", b)], writes=[("ostg", oi)])
                    dd, roff = dsts[jj]
                    r0 = roff + jj * 128
                    S.dma("pool", dd[r0:r0 + 128, t0:t0 + TT], ostg[oi][:, jj, :], reads=[("ostg", oi)], group=("sto", oi))
            for vi, (n0, ncols, dd) in enumerate(((3840, 256, vs_o), (4352, 256, vw_o), (4608, 48, gt_o))):
                P.proj_tm(win, 0, NCH, n0, lambda c, s: hT[:, c, s * 128:(s + 1) * 128], ("hT", 0), True, True, ncols=ncols)
                for s in range(NS):
                    if ncols == 48:
                        S.op("act", lambda e, s=s: e.activation(out=gst[:, s, :], in_=P.bank[s][:, 0:48], func=AF.Sigmoid),
                             reads=[("bank", s)], writes=["gst"])
                    else:
                        S.op("act", lambda e, s=s, vi=vi: e.activation(out=vst[vi][:, s, :], in_=P.bank[s][:, 0:256], func=AF.Copy),
                             reads=[("bank", s)], writes=[("vst", vi)])
                if ncols == 48:
                    S.dma("pool", dd[t0:t0 + TT, :].rearrange("(s p) c -> p s c", p=128), gst[:], reads=["gst"], group="st_g")
                else:
                    S.dma("pool", dd[t0:t0 + TT, :].rearrange("(s p) c -> p s c", p=128), vst[vi][:], reads=[("vst", vi)], group=("stv", vi))
            ND = CONV_W
            for cc in range(NCC):
                i = cc % 2
                S.op("dve", lambda e, cc=cc: e.tensor_scalar(acc[cc][:], gl[cc][:, 0:TT], cw[:, cc, 0:1], cb[:, cc:cc + 1], ALU.mult, ALU.add),
                     reads=[("gl", cc), "cw", "cb"], writes=[("acc", cc)])
                for k in range(1, ND):
                    S.op("dve", lambda e, cc=cc, k=k: e.scalar_tensor_tensor(acc[cc][:], gl[cc][:, k:k + TT], cw[:, cc, k:k + 1], acc[cc][:],
                                                                             ALU.mult, ALU.add),
                         reads=[("gl", cc), ("acc", cc), "cw"], writes=[("acc", cc)])
                S.op("pool", lambda e, cc=cc: e.tensor_copy(gl[cc][:, 0:30], gl[cc][:, TT:GW]), reads=[("gl", cc)], writes=[("gl", cc)])
                S.op("act", lambda e, cc=cc, i=i: e.activation(out=abf[i][:], in_=acc[cc][:], func=AF.Copy),
                     reads=[("acc", cc)], writes=[("abf", i)])
                S.op("act", lambda e, cc=cc, i=i: e.activation(out=asq[i][:], in_=acc[cc][:], func=AF.Square),
                     reads=[("acc", cc)], writes=[("asq", i)])
                S.op("pe", lambda e, cc=cc, i=i: e.matmul(P.bank[0][:], ones[:], abf[i][:], start=(cc == 0), stop=(cc == NCC - 1)),
                     reads=["ones", ("abf", i)], writes=[("bank", 0)])
                S.op("pe", lambda e, cc=cc, i=i: e.matmul(P.bank[1][:], ones[:], asq[i][:], start=(cc == 0), stop=(cc == NCC - 1)),
                     reads=["ones", ("asq", i)], writes=[("bank", 1)])
            S.op("dve", lambda e: e.tensor_scalar(mean[:], P.bank[0][:], 1.0 / CONV_CH, None, ALU.mult), reads=[("bank", 0)], writes=["mean"])
            S.op("dve", lambda e: e.tensor_tensor(msq[:], mean[:], mean[:], ALU.mult), reads=["mean"], writes=["msq"])
            S.op("dve", lambda e: e.scalar_tensor_tensor(rstd[:], P.bank[1][:], 1.0 / CONV_CH, msq[:], ALU.mult, ALU.subtract),
                 reads=[("bank", 1), "msq"], writes=["rstd"])
            S.op("dve", lambda e: e.tensor_scalar(rstd[:], rstd[:], EPS, None, ALU.add), reads=["rstd"], writes=["rstd"])
            S.op("act", lambda e: e.activation(out=rstd[:], in_=rstd[:], func=AF.Sqrt), reads=["rstd"], writes=["rstd"])
            S.op("dve", lambda e: e.reciprocal(rstd[:], rstd[:]), reads=["rstd"], writes=["rstd"])
            for cc in range(NCC):
                oi = (cc // 4) % 2
                S.op("dve", lambda e, cc=cc: e.tensor_tensor(acc[cc][:], acc[cc][:], mean[:], ALU.subtract),
                     reads=[("acc", cc), "mean"], writes=[("acc", cc)])
                S.op("dve", lambda e, cc=cc: e.tensor_tensor(acc[cc][:], acc[cc][:], rstd[:], ALU.mult),
                     reads=[("acc", cc), "rstd"], writes=[("acc", cc)])
                S.op("act", lambda e, cc=cc, oi=oi: e.activation(out=ostg[oi][:, cc % 4, :], in_=acc[cc][:], func=AF.Silu,
                                                                 scale=lw[:, cc:cc + 1], bias=lbb[:, cc:cc + 1]),
                     reads=[("acc", cc), "lw", "lbb"], writes=[("ostg", oi)])
                if cc % 4 == 3:
                    r0 = (cc // 4) * 512
                    S.dma("pool", aT_o[r0:r0 + 512, t0:t0 + TT].rearrange("(c p) t -> p c t", p=128), ostg[oi][:],
                          reads=[("ostg", oi)], group=("sto", oi))
        S.emit()
    return nc


NEG = -30000.0
import os as _os
_NOSEL = bool(_os.environ.get('NSA_NOSEL'))
TINY = 1e-30


def nsa_consts(g, Sq):
    NKT = Sq // 128
    NC = Sq // 16 - 1
    NCT = (NC + 127) // 128
    bf = NPBF
    slopes = (2.0 ** (-8.0 * (np.arange(1, 17, dtype=np.float64)) / 16.0)).reshape(4, 4)[g]
    s_hi = slopes.astype(np.float32).astype(bf).astype(np.float64)
    s_lo = (slopes - s_hi).astype(np.float32).astype(bf).astype(np.float64)
    qstat = np.zeros((8, 4, 128), np.float32)
    for h in range(4):
        for r, mul in enumerate((1.0, 128.0, 16.0, 2048.0)):
            qstat[2 * r, h, :] = s_hi[h] * mul
            qstat[2 * r + 1, h, :] = s_lo[h] * mul
    c = {}
    c["qstat"] = qstat.reshape(8, 512).astype(bf)
    kst = np.zeros((33, Sq), np.float32)
    pos = np.arange(Sq)
    kst[0] = 1.0
    kst[1] = kst[2] = pos % 128
    kst[3] = kst[4] = pos // 128
    kst[32] = 1.0
    c["kstat"] = kst.astype(bf)
    kss = np.zeros((64, Sq), np.float32)
    kss[:33] = kst
    jj = (pos // 64) % 30
    kss[34 + jj, pos] = -NEG
    c["kstat_s"] = kss.astype(bf)
    cst = np.zeros((33, NCT * 128), np.float32)
    n = np.arange(NCT * 128)
    cst[0] = 1.0
    cst[5] = cst[6] = n % 128
    cst[7] = cst[8] = n // 128
    cst[32] = 1.0
    c["cstat"] = cst.astype(bf)
    kp = np.arange(128)[:, None]
    tp = np.arange(128)[None, :]
    causal = np.where(kp <= tp, 0.0, NEG).astype(np.float32)
    upper = np.where(kp > tp, 0.0, NEG).astype(np.float32)
    c["causal"] = np.tile(causal, (1, 4)).astype(bf)
    c["upper"] = np.tile(upper, (1, 4)).astype(bf)
    cm = np.zeros((128, 17, 4, 128), np.float32)
    for d in range(17):
        m = np.where(tp + 128 * d >= 16 * kp + 31, 0.0, NEG)
        cm[:, d, :, :] = m[:, None, :]
    c["cmask"] = cm.reshape(128, 17 * 512).astype(bf)
    ov = np.zeros((128, NCT, 256), np.float32)
    for nn in range(NC):
        for j in range(Sq // 64):
            if 4 * j - 1 <= nn <= 4 * j + 3:
                ov[nn % 128, nn // 128, j] = 1.0
    c["ovl"] = ov.reshape(128, NCT * 256).astype(bf)
    rb = np.zeros((128, 4, 128), np.float32)
    sl = np.zeros((128, 4, 128), np.float32)
    for h in range(4):
        rb[:, h, :] = -(slopes[h] * np.arange(128))[None, :]
        sl[:, h, :] = slopes[h]
    c["rbase"] = rb.reshape(128, 512)
    c["sl512"] = sl.reshape(128, 512)
    keep = np.zeros((128, 3), np.float32)
    add = np.zeros((128, 3), np.float32)
    keep[64:, 0] = 1.0
    add[:64] = [1000.0, 1000.0, -1.0]
    add[64:] = [0.0, 1000.0, 1000.0]
    c["keep3"] = keep
    c["add3"] = add
    c["ident"] = np.eye(128, dtype=np.float32).astype(bf)
    return c


NSA_CONST_SPECS = lambda Sq: {
    "qstat": ([8, 512], BF16), "kstat": ([33, Sq], BF16), "cstat": ([33, ((Sq // 16 - 1 + 127) // 128) * 128], BF16),
    "causal": ([128, 512], BF16), "upper": ([128, 512], BF16), "cmask": ([128, 17 * 512], BF16),
    "kstat_s": ([64, Sq], BF16), "ovl": ([128, ((Sq // 16 - 1 + 127) // 128) * 256], BF16),
    "rbase": ([128, 512], F32), "sl512": ([128, 512], F32), "keep3": ([128, 3], F32), "add3": ([128, 3], F32),
    "ident": ([128, 128], BF16)}


def build_nsa(Sq):
    nc = new_nc()
    dt = nc.dram_tensor
    NQ = Sq // 128
    NKT = Sq // 128
    NC = Sq // 16 - 1
    NCT = (NC + 127) // 128
    NCP = NCT * 128
    NSB = Sq // 64
    inp = lambda nm, shp, d: dt(nm, shp, d, kind="ExternalInput").ap()
    qT = inp("qT", [256, Sq], BF16)
    kcT = inp("kcT", [64, Sq], BF16)
    vcT = inp("vcT", [64, Sq], BF16)
    ksT = inp("ksT", [64, Sq], BF16)
    kwT = inp("kwT", [64, Sq], BF16)
    vsd = inp("vs", [Sq, 64], BF16)
    vwd = inp("vw", [Sq, 64], BF16)
    gtd = inp("gates", [Sq, 12], F32)
    posd = inp("cpos", [2, 32, 64], F32)
    w1d = inp("cw1", [2, 2048, 256], F32)
    b1d = inp("cb1", [2, 256], F32)
    w2d = inp("cw2", [2, 256, 64], F32)
    b2d = inp("cb2", [2, 64], F32)
    cd = {k: inp(k, shp, d) for k, (shp, d) in NSA_CONST_SPECS(Sq).items()}
    bT = dt("bT", [256, Sq], BF16, kind="ExternalOutput").ap()

    with ExitStack() as ctx:
        S = Sched(nc, ctx)
        bank = [S.ps(f"bank{i}", [128, 512], F32) for i in range(8)]
        BX = [0, 1, 2]
        B_OC, B_OS, B_OW, B_I0, B_I1 = 3, 4, 5, 6, 7
        cst = {}
        for k, (shp, d) in NSA_CONST_SPECS(Sq).items():
            if k in ("kstat", "cstat", "qstat", "kstat_s"):
                continue
            cst[k] = S.sb("c_" + k, shp, d)
            S.dma("sp", cst[k][:], cd[k], writes=["c_" + k])
        ident = cst["ident"]
        ones = S.sb("ones_sb", [128, 128], BF16)
        S.op("pool", lambda e: e.memset(ones[:], 1.0), writes=["ones"])
        Ka_s = S.sb("Ka_s", [128, Sq], BF16)
        Ka_w = S.sb("Ka_w", [128, Sq], BF16)
        Ka_c = S.sb("Ka_c", [128, NCP], BF16)
        Va_s = S.sb("Va_s", [128, NKT, 65], BF16)
        Va_w = S.sb("Va_w", [128, NKT, 65], BF16)
        Va_c = S.sb("Va_c", [128, NCT, 65], BF16)
        kmax2 = S.sb("kmax2", [128, 1], F32)
        kmt = S.sb("kmt", [128, 1], F32)
        ksq = [S.sb(f"ksq{i}", [64, 512], BF16) for i in range(2)]
        S.op("pool", lambda e: e.memset(kmax2[:], 0.0), writes=["kmax2"])
        S.op("pool", lambda e: e.memset(Ka_c[:], 0.0), writes=["Ka_c"])
        S.op("pool", lambda e: e.memset(Va_c[:], 0.0), writes=["Va_c"])
        S.op("pool", lambda e: e.memset(Va_s[:], 1.0), writes=["Va_s"])
        S.op("pool", lambda e: e.memset(Va_w[:], 1.0), writes=["Va_w"])
        S.dma("sp", Ka_s[64:128, :], cd["kstat_s"], writes=["Ka_s"])
        S.dma("sp", Ka_c[64:97, :], cd["cstat"], writes=["Ka_c"])
        S.dma("sp", Ka_s[0:64, :], ksT, writes=["Ka_s"])
        for k0 in range(0, NKT, 32):
            k1 = min(NKT, k0 + 32)
            S.dma("pool", Va_s[:, k0:k1, 0:64], vsd[k0 * 128:k1 * 128, :].rearrange("(k p) d -> p k d", p=128), writes=["Va_s"])
            S.dma("pool", Va_w[:, k0:k1, 0:64], vwd[k0 * 128:k1 * 128, :].rearrange("(k p) d -> p k d", p=128), writes=["Va_w"])
        S.op("pool", lambda e: e.memset(Va_c[:, :, 64:65], 1.0), writes=["Va_c"])

        bxi = [0]

        def nextbx():
            b = BX[bxi[0] % len(BX)]
            bxi[0] += 1
            return b

        def kmax_update(K_, nm, ncols):
            for c0 in range(0, ncols, 512):
                n = min(512, ncols - c0)
                i = (c0 // 512) % 2
                b = nextbx()
                S.op("act", lambda e, i=i, c0=c0, n=n: e.activation(out=ksq[i][:, 0:n], in_=K_[0:64, c0:c0 + n], func=AF.Square),
                     reads=[nm], writes=[("ksq", i)])
                S.op("pe", lambda e, b=b, i=i, n=n: e.matmul(bank[b][:, 0:n], ones[0:64, :], ksq[i][:, 0:n], start=True, stop=True),
                     reads=["ones", ("ksq", i)], writes=[("bank", b)])
                S.op("dve", lambda e, b=b, n=n: e.tensor_reduce(out=kmt[:], in_=bank[b][:, 0:n], axis=AX.X, op=ALU.max),
                     reads=[("bank", b)], writes=["kmt"])
                S.op("dve", lambda e: e.tensor_tensor(kmax2[:], kmax2[:], kmt[:], ALU.max), reads=["kmt", "kmax2"], writes=["kmax2"])

        kmax_update(Ka_s, "Ka_s", Sq)

        w1s = S.sb("w1s", [64, 8, 256], F32)
        w1b = S.sb("w1b", [64, 32, 256], BF16)
        w2s = S.sb("w2s", [128, 2, 64], F32)
        w2b = S.sb("w2b", [128, 2, 64], BF16)
        posf = S.sb("posf", [64, 32], F32)
        posb = S.sb("posb", [64, 32], BF16)
        b1t = S.sb("b1t", [128, 2], F32)
        c1 = S.sb("c1", [128, 2], F32)
        b2k = S.sb("b2k", [64, 1], F32)
        b2v = S.sb("b2v", [128, 64], F32)
        hact = S.sb("hact", [128, 2, NCP], BF16)
        S.op("pool", lambda e: e.memset(hact[:], 0.0), writes=["hact"])
        kc3 = Ka_w[0:64, :].rearrange("d (n s) -> d n s", s=16)
        for e_, srcT in ((0, kcT), (1, vcT)):
            S.dma("sp", Ka_w[0:64, :], srcT, writes=["Ka_w"])
            for q4 in range(4):
                S.dma("pool", w1s[:], w1d[e_, q4 * 512:(q4 + 1) * 512, :].rearrange("(i d) n -> d i n", d=64), writes=["w1s"])
                S.op("dve", lambda e, q4=q4: e.tensor_copy(w1b[:, q4 * 8:(q4 + 1) * 8, :], w1s[:]), reads=["w1s"], writes=["w1b"])
            S.dma("pool", w2s[:], w2d[e_].rearrange("(c p) d -> p c d", p=128), writes=["w2s"])
            S.op("dve", lambda e: e.tensor_copy(w2b[:], w2s[:]), reads=["w2s"], writes=["w2b"])
            S.dma("pool", posf[:], posd[e_].rearrange("i d -> d i"), writes=["posf"], allow_slow_non_contiguous=True)
            S.op("dve", lambda e: e.tensor_copy(posb[:], posf[:]), reads=["posf"], writes=["posb"])
            S.dma("pool", b1t[:], b1d[e_].rearrange("(c p) -> p c", p=128), writes=["b1t"], allow_slow_non_contiguous=True)
            if e_ == 0:
                S.dma("pool", b2k[:], b2d[0].rearrange("(d o) -> d o", o=1), writes=["b2k"], allow_slow_non_contiguous=True)
            else:
                S.dma("pool", b2v[:], b2d[1].partition_broadcast(128), writes=["b2v"])
            for hc in range(2):
                b = nextbx()
                for i in range(32):
                    S.op("pe", lambda e, b=b, i=i, hc=hc: e.matmul(bank[b][:, 0:1], w1b[:, i, hc * 128:(hc + 1) * 128], posb[:, i:i + 1],
                                                                   start=(i == 0), stop=(i == 31)),
                         reads=["w1b", "posb"], writes=[("bank", b)])
                S.op("dve", lambda e, b=b, hc=hc: e.tensor_tensor(c1[:, hc:hc + 1], bank[b][:, 0:1], b1t[:, hc:hc + 1], ALU.add),
                     reads=[("bank", b), "b1t"], writes=["c1"])
            for n0 in range(0, NC, 512):
                nn = min(512, NC - n0)
                for hc in range(2):
                    b = nextbx()
                    for i in range(32):
                        rhs = kc3[:, n0:n0 + nn, i] if i < 16 else kc3[:, n0 + 1:n0 + 1 + nn, i - 16]
                        S.op("pe", lambda e, b=b, i=i, hc=hc, rhs=rhs, nn=nn: e.matmul(bank[b][:, 0:nn], w1b[:, i, hc * 128:(hc + 1) * 128], rhs,
                                                                                       start=(i == 0), stop=(i == 31)),
                             reads=["w1b", "Ka_w"], writes=[("bank", b)])
                    S.op("act", lambda e, b=b, hc=hc, n0=n0, nn=nn: e.activation(out=hact[:, hc, n0:n0 + nn], in_=bank[b][:, 0:nn], func=AF.Silu,
                                                                                 bias=c1[:, hc:hc + 1]),
                         reads=[("bank", b), "c1"], writes=["hact"])
            if e_ == 0:
                for n0 in range(0, NCP, 512):
                    nn = min(512, NCP - n0)
                    b = nextbx()
                    for hc in range(2):
                        S.op("pe", lambda e, b=b, hc=hc, n0=n0, nn=nn: e.matmul(bank[b][0:64, 0:nn], w2b[:, hc, :], hact[:, hc, n0:n0 + nn],
                                                                               start=(hc == 0), stop=(hc == 1)),
                             reads=["w2b", "hact"], writes=[("bank", b)])
                    S.op("act", lambda e, b=b, n0=n0, nn=nn: e.activation(out=Ka_c[0:64, n0:n0 + nn], in_=bank[b][0:64, 0:nn], func=AF.Identity,
                                                                          bias=b2k[:, 0:1]),
                         reads=[("bank", b), "b2k"], writes=["Ka_c"])
                if NCP > NC:
                    S.op("pool", lambda e: e.memset(Ka_c[0:64, NC:NCP], 0.0), writes=["Ka_c"])
                kmax_update(Ka_c, "Ka_c", NCP)
            else:
                for nb in range(NCT):
                    b = nextbx()
                    for hc in range(2):
                        S.op("pe", lambda e, b=b, hc=hc, nb=nb: e.matmul(bank[b][:, 0:64], hact[:, hc, nb * 128:(nb + 1) * 128], w2b[:, hc, :],
                                                                        start=(hc == 0), stop=(hc == 1)),
                             reads=["w2b", "hact"], writes=[("bank", b)])
                    S.op("dve", lambda e, b=b, nb=nb: e.tensor_tensor(Va_c[:, nb, 0:64], bank[b][:, 0:64], b2v[:], ALU.add),
                         reads=[("bank", b), "b2v"], writes=["Va_c"])
        S.dma("sp", Ka_w[0:64, :], kwT, writes=["Ka_w"])
        S.op("pool", lambda e: e.memset(Ka_w[64:128, :], 0.0), writes=["Ka_w"])
        S.dma("sp", Ka_w[64:97, :], cd["kstat"], writes=["Ka_w"])
        kmax_update(Ka_w, "Ka_w", Sq)
        S.op("act", lambda e: e.activation(out=kmax2[:], in_=kmax2[:], func=AF.Sqrt, scale=1.05), reads=["kmax2"], writes=["kmax2"])
        S.op("dve", lambda e: e.tensor_scalar(kmax2[:], kmax2[:], 0.5, None, ALU.mult), reads=["kmax2"], writes=["kmax2"])

        Qa = [S.sb(f"Qa{i}", [128, 512], BF16) for i in range(2)]
        for i in range(2):
            S.op("pool", lambda e, i=i: e.memset(Qa[i][:], 0.0), writes=[("Qa", i)])
            S.dma("sp", Qa[i][65:73, :], cd["qstat"], writes=[("Qa", i)])
        qs = S.sb("qs", [64, 512], BF16)
        mrow = S.sb("mrow", [128, 512], F32)
        rhi = S.sb("rhi", [128, 512], BF16)
        rlo2 = [S.sb(f"rlo{i}", [128, 512], BF16) for i in range(2)]
        gt = [S.sb(f"gt{i}", [128, 12], F32) for i in range(2)]
        pTc = S.sb("pTc", [128, NCT, 512], BF16)
        pT = [S.sb(f"pT{i}", [128, 512], BF16) for i in range(4)]
        rinv = S.sb("rinv", [128, 3, 4], F32)
        coef = S.sb("coef", [128, 3, 4], F32)
        NWIN = (NSB + 29) // 30
        imp = S.sb("imp", [128, NWIN * 30], F32)
        imp2 = S.sb("imp2", [128, NWIN * 30], F32)
        selbW = S.sb("selbW", [128, NWIN, 128], BF16)
        selW = S.sb("selW", [128, NWIN, 512], BF16)
        S.op("pool", lambda e: e.memset(imp[:], -1.0), writes=["imp"])
        S.op("pool", lambda e: e.memset(imp2[:], -1.0), writes=["imp2"])
        S.op("pool", lambda e: e.memset(selbW[:], 0.0), writes=["selbW"])
        S.op("pool", lambda e: e.memset(selW[:], 0.0), writes=["selW"])
        m8a = S.sb("m8a", [128, 8], F32)
        m8b = S.sb("m8b", [128, 8], F32)
        ofin = S.sb("ofin", [128, 4, 64], F32)
        otmp = S.sb("otmp", [128, 4, 64], F32)
        obf = S.sb("obf", [128, 256], BF16)
        obT = [S.sb(f"obT{i}", [128, 2, 512], BF16) for i in range(2)]
        pti = [0]
        pend = [None]

        def attend_tile(qi, Ka, kaname, col0, Va, vaname, vidx, masks, obank, first, last):
            b = nextbx()
            S.op("pe", lambda e, b=b: e.matmul(bank[b][:], Ka[0:128, col0:col0 + 128], Qa[qi][0:128, :], start=True, stop=(len(masks) == 0)),
                 reads=[kaname, ("Qa", qi)], writes=[("bank", b)])
            for mi, (l_, r_, toks) in enumerate(masks):
                S.op("pe", lambda e, b=b, l_=l_, r_=r_, mi=mi: e.matmul(bank[b][:], l_, r_, start=False, stop=(mi == len(masks) - 1)),
                     reads=list(toks), writes=[("bank", b)])
            return b

        def pv(src, srctok, Va, vaname, vidx, obank, first, last):
            for h in range(4):
                S.op("pe", lambda e, h=h: e.matmul(bank[obank][:, h * 65:(h + 1) * 65], src[:, h * 128:(h + 1) * 128], Va[:, vidx, :],
                                                  start=(first and h == 0), stop=last, skip_group_check=True),
                     reads=[srctok, vaname], writes=[("bank", obank)])

        def finish_branch(br, obank, qi_g, first_branch):
            o3 = bank[obank][:, 0:260].rearrange("p (h c) -> p h c", c=65)
            S.op("dve", lambda e: e.tensor_scalar(rinv[:, br, :], o3[:, :, 64], TINY, None, ALU.max),
                 reads=[("bank", obank)], writes=[("rinv", br)])
            S.op("dve", lambda e: e.reciprocal(rinv[:, br, :], rinv[:, br, :]), reads=[("rinv", br)], writes=[("rinv", br)])
            S.op("dve", lambda e: e.tensor_tensor(coef[:, br, :], rinv[:, br, :], gt[qi_g][:, br * 4:(br + 1) * 4], ALU.mult),
                 reads=[("rinv", br), ("gt", qi_g)], writes=[("coef", br)])
            cb_ = coef[:, br, :].unsqueeze(2).to_broadcast([128, 4, 64])
            if first_branch:
                S.op("dve", lambda e: e.tensor_tensor(ofin[:], o3[:, :, 0:64], cb_, ALU.mult),
                     reads=[("bank", obank), ("coef", br)], writes=["ofin"])
            else:
                S.op("dve", lambda e: e.tensor_tensor(otmp[:], o3[:, :, 0:64], cb_, ALU.mult),
                     reads=[("bank", obank), ("coef", br)], writes=["otmp"])
                S.op("dve", lambda e: e.tensor_tensor(ofin[:], ofin[:], otmp[:], ALU.add), reads=["ofin", "otmp"], writes=["ofin"])

        def prep(qb):
            t0 = qb * 128
            qi = qb % 2
            S.dma("sp", Qa[qi][0:64, :].rearrange("d (h t) -> d h t", h=4), qT[:, t0:t0 + 128].rearrange("(h d) t -> d h t", d=64),
                  writes=[("Qa", qi)])
            S.dma("sp", gt[qi][:], gtd[t0:t0 + 128, :], writes=[("gt", qi)])
            S.op("dve", lambda e: e.tensor_tensor(qs[:], Qa[qi][0:64, :], Qa[qi][0:64, :], ALU.mult), reads=[("Qa", qi)], writes=["qs"])
            b = nextbx()
            S.op("pe", lambda e: e.matmul(bank[b][:], ones[0:64, :], qs[:], start=True, stop=True), reads=["ones", "qs"],
                 writes=[("bank", b)])
            S.op("dve", lambda e: e.tensor_scalar(mrow[64:128, :], bank[b][64:128, :], kmax2[64:128, 0:1], kmax2[64:128, 0:1], ALU.mult, ALU.add),
                 reads=[("bank", b), "kmax2"], writes=["mrow"])
            S.op("dve", lambda e: e.scalar_tensor_tensor(mrow[64:128, :], mrow[64:128, :], -1.0, cst["rbase"][64:128, :], ALU.mult, ALU.add),
                 reads=["mrow", "c_rbase"], writes=["mrow"])
            S.op("dve", lambda e: e.scalar_tensor_tensor(mrow[64:128, :], cst["sl512"][64:128, :], -float(t0), mrow[64:128, :],
                                                         ALU.mult, ALU.add),
                 reads=["mrow", "c_sl512"], writes=["mrow"])
            S.op("dve", lambda e: e.tensor_copy(rhi[64:128, :], mrow[64:128, :]), reads=["mrow"], writes=["rhi"])
            S.op("dve", lambda e: e.tensor_tensor(rlo2[qi][64:128, :], mrow[64:128, :], rhi[64:128, :], ALU.subtract),
                 reads=["mrow", "rhi"], writes=[("rlo", qi)])
            S.op("dve", lambda e: e.tensor_copy(Qa[qi][64:65, :], rhi[64:65, :]), reads=["rhi"], writes=[("Qa", qi)])
            S.op("dve", lambda e: e.tensor_copy(Qa[qi][96:97, :], rlo2[qi][96:97, :]), reads=[("rlo", qi)], writes=[("Qa", qi)])

        prep(0)
        for qb in range(NQ):
            t0 = qb * 128
            qi = qb % 2
            def tile_job(Ka, kaname, col0, masks, dst, dtok, after):
                b = attend_tile(qi, Ka, kaname, col0, None, None, None, masks, None, None, None)
                S.op("act", lambda e, b=b, dst=dst: e.activation(out=dst, in_=bank[b][:], func=AF.Exp), reads=[("bank", b)], writes=[dtok])
                if pend[0] is not None:
                    f_ = pend[0]
                    pend[0] = None
                    f_()
                pend[0] = after

            nmax = (t0 + 96) // 16
            tiles_c = list(range(0, min(NCT - 1, nmax // 128) + 1))
            nv = 2 * qb + 2
            if _NOSEL:
                nv = 0
            for ci, nb in enumerate(tiles_c):
                full = 16 * (128 * nb + 127) + 31 <= t0
                masks = []
                if not full:
                    d_ = qb - 16 * nb
                    masks = [(ident[:], cst["cmask"][:, d_ * 512:(d_ + 1) * 512], ["c_ident", "c_cmask"])]
                first, last = ci == 0, ci == len(tiles_c) - 1

                def after_c(nb=nb, first=first, last=last, qi=qi, qb=qb, nv=nv):
                    pv(pTc[:, nb, :], "pTc", Va_c, "Va_c", nb, B_OC, first, last)
                    for h in range(4):
                        ib = B_I0 if h < 2 else B_I1
                        S.op("pe", lambda e, h=h, ib=ib: e.matmul(
                            bank[ib][:, (h % 2) * 256:(h % 2 + 1) * 256], pTc[:, nb, h * 128:(h + 1) * 128],
                            cst["ovl"][:, nb * 256:(nb + 1) * 256], start=(first and h % 2 == 0), stop=last, skip_group_check=True),
                            reads=["pTc", "c_ovl"], writes=[("bank", ib)])
                    if not last:
                        return
                    finish_branch(0, B_OC, qi, True)
                    if nv > 16:
                        for h in range(4):
                            ib = B_I0 if h < 2 else B_I1
                            src_ = bank[ib][:, (h % 2) * 256:(h % 2 + 1) * 256]
                            if h == 0:
                                S.op("dve", lambda e, src_=src_: e.tensor_scalar(imp[:, 0:NSB], src_[:, 0:NSB], rinv[:, 0, 0:1], None, ALU.mult),
                                     reads=[("bank", ib), ("rinv", 0)], writes=["imp"])
                            else:
                                S.op("dve", lambda e, src_=src_, h=h: e.scalar_tensor_tensor(imp[:, 0:NSB], src_[:, 0:NSB], rinv[:, 0, h:h + 1],
                                                                                             imp[:, 0:NSB], ALU.mult, ALU.add),
                                     reads=[("bank", ib), ("rinv", 0), "imp"], writes=["imp"])
                        S.op("dve", lambda e: e.tensor_scalar(imp[:, 0:1], imp[:, 0:1], 0.0, 1000.0, ALU.mult, ALU.add), reads=["imp"], writes=["imp"])
                        c0 = 2 * qb - 1
                        S.op("dve", lambda e: e.tensor_tensor(imp[:, c0:c0 + 3], imp[:, c0:c0 + 3], cst["keep3"][:], ALU.mult),
                             reads=["imp", "c_keep3"], writes=["imp"])
                        S.op("dve", lambda e: e.tensor_tensor(imp[:, c0:c0 + 3], imp[:, c0:c0 + 3], cst["add3"][:], ALU.add),
                             reads=["imp", "c_add3"], writes=["imp"])
                        nvp = min(NSB, (nv + 3) // 4 * 4)
                        S.op("dve", lambda e: e.max(m8a[:], imp[:, 0:nvp]), reads=["imp"], writes=["m8a"])
                        S.op("dve", lambda e: e.match_replace(imp2[:, 0:nvp], m8a[:], imp[:, 0:nvp], -1e30), reads=["imp", "m8a"], writes=["imp2"])
                        S.op("dve", lambda e: e.max(m8b[:], imp2[:, 0:nvp]), reads=["imp2"], writes=["m8b"])
                        nwq = qb // 15 + 1
                        S.op("dve", lambda e: e.tensor_scalar(selbW[:, 0:nwq, 98:128], imp[:, 0:nwq * 30].rearrange("p (w j) -> p w j", j=30),
                                                              m8b[:, 7:8], 1.0, ALU.is_ge, ALU.subtract),
                             reads=["imp", "m8b"], writes=["selbW"])
                tile_job(Ka_c, "Ka_c", nb * 128, masks, pTc[:, nb, :], "pTc", after_c)

            tiles_w = list(range(max(0, qb - 4), qb + 1))
            for wi_, kb in enumerate(tiles_w):
                masks = []
                if kb == qb:
                    masks.append((ident[:], cst["causal"][:], ["c_ident", "c_causal"]))
                if kb == qb - 4:
                    masks.append((ident[:], cst["upper"][:], ["c_ident", "c_upper"]))
                pi = pti[0] % 4
                pti[0] += 1

                def after_w(pi=pi, kb=kb, first=(wi_ == 0), last=(wi_ == len(tiles_w) - 1), qi=qi):
                    pv(pT[pi], ("pT", pi), Va_w, "Va_w", kb, B_OW, first, last)
                    if last:
                        finish_branch(2, B_OW, qi, False)
                tile_job(Ka_w, "Ka_w", kb * 128, masks, pT[pi][:], ("pT", pi), after_w)

            if qb + 1 < NQ:
                prep(qb + 1)
            if nv > 16:
                for w in range(qb // 15 + 1):
                    b = nextbx()
                    pTv = bank[b][:].bitcast(BF16)[:, 0:128]
                    S.op("pe", lambda e, pTv=pTv, w=w: e.transpose(pTv, selbW[:, w, :], ident[:]),
                         reads=["selbW", "c_ident"], writes=[("bank", b)])
                    S.op("dve", lambda e, pTv=pTv, w=w: e.tensor_copy(selW[96:128, w, :].rearrange("p (h t) -> p h t", h=4),
                                                                    pTv[96:128, :].unsqueeze(1).to_broadcast([32, 4, 128])),
                         reads=[("bank", b)], writes=["selW"])

            for kb in range(qb + 1):
                masks = []
                if nv > 16 and kb % 15 == 0:
                    w = kb // 15
                    S.op("dve", lambda e, w=w, qi=qi: e.tensor_copy(Qa[qi][96:128, :], selW[96:128, w, :]), reads=["selW"], writes=[("Qa", qi)])
                    S.op("dve", lambda e, qi=qi: e.tensor_copy(Qa[qi][96:97, :], rlo2[qi][96:97, :]), reads=[("rlo", qi)], writes=[("Qa", qi)])
                if kb == qb:
                    masks.append((ident[:], cst["causal"][:], ["c_ident", "c_causal"]))
                pi = pti[0] % 4
                pti[0] += 1

                def after_s(pi=pi, kb=kb, qb=qb, qi=qi):
                    pv(pT[pi], ("pT", pi), Va_s, "Va_s", kb, B_OS, kb == 0, kb == qb)
                    if kb != qb:
                        return
                    finish_branch(1, B_OS, qi, False)
                    oi = (qb // 4) % 2
                    S.op("dve", lambda e: e.tensor_copy(obf[:], ofin[:].rearrange("p h d -> p (h d)")),
                         reads=["ofin"], writes=["obf"])
                    for c2 in range(2):
                        b = nextbx()
                        pTv = bank[b][:].bitcast(BF16)[:, 0:128]
                        S.op("pe", lambda e, pTv=pTv, c2=c2: e.transpose(pTv, obf[:, c2 * 128:(c2 + 1) * 128], ident[:]),
                             reads=["obf", "c_ident"], writes=[("bank", b)])
                        S.op("dve", lambda e, pTv=pTv, c2=c2: e.tensor_copy(obT[oi][:, c2, (qb % 4) * 128:(qb % 4 + 1) * 128], pTv),
                             reads=[("bank", b)], writes=[("obT", oi)])
                    if qb % 4 == 3:
                        tq = (qb - 3) * 128
                        S.dma("pool", bT[:, tq:tq + 512].rearrange("(c p) t -> p c t", p=128), obT[oi][:], reads=[("obT", oi)],
                              group=("sto", oi))
                tile_job(Ka_s, "Ka_s", kb * 128, masks, pT[pi][:], ("pT", pi), after_s)
        if pend[0] is not None:
            pend[0]()
            pend[0] = None
        S.emit()
    return nc


_NC_CACHE = {}


def _get(name, fn, *a):
    key = (name,) + tuple(a)
    if key not in _NC_CACHE:
        _NC_CACHE[key] = fn(*a)
    return _NC_CACHE[key]


def _run(nc, in_maps):
    res = run_bass_kernel_spmd(nc, in_maps, core_ids=list(range(N_CORES)))
    return res.results


def _convert(mats, scales):
    N = mats[0].shape[1]
    padded, ks = [], []
    for m in mats:
        K_ = m.shape[0]
        Kp = ((K_ + 1023) // 1024) * 1024
        if Kp != K_:
            m = np.concatenate([m, np.zeros((Kp - K_, N), np.float32)], axis=0)
        padded.append(m)
        ks.append((K_, Kp))
    big = np.concatenate(padded, axis=0)
    with_scale = scales is not None
    if with_scale:
        sc = np.concatenate([np.concatenate([s, np.ones(kp - k, np.float32)]) for s, (k, kp) in zip(scales, ks)])
    R = big.shape[0] // N_CORES
    nc = _get("conv", build_convert, R, N, with_scale)
    ins = []
    for c in range(N_CORES):
        d = {"w": np.ascontiguousarray(big[c * R:(c + 1) * R])}
        if with_scale:
            d["sc"] = np.ascontiguousarray(sc[c * R:(c + 1) * R])
        ins.append(d)
    res = _run(nc, ins)
    full = np.concatenate([np.asarray(r["wb"]) for r in res], axis=0)
    outs, off = [], 0
    for (k, kp) in ks:
        outs.append(np.ascontiguousarray(full[off:off + k]))
        off += kp
    return outs


def kernel(x, norm_w, final_norm_w, ev_w_in, ev_conv_w, ev_conv_b, ev_conv_ln_w, ev_conv_ln_b,
           ev_cmp_pos, ev_cmp_w1, ev_cmp_b1, ev_cmp_w2, ev_cmp_b2, ev_w_out,
           od_w_in, od_lb_gamma, od_gnorm_w, od_w_out, ffn_w_gu, ffn_w_down):
    f32 = np.float32
    x = np.asarray(x, f32)
    B, Sq, _ = x.shape
    RPB = N_CORES // B
    Tc = Sq // RPB
    norm_w = np.asarray(norm_w, f32)
    ident = np.eye(128, dtype=f32).astype(NPBF)

    (win1,) = _convert([np.asarray(ev_w_in[0], f32)], [norm_w[0, 0]])
    wo0, wo1 = _convert([np.asarray(ev_w_out[0], f32), np.asarray(od_w_out[0], f32)], None)
    wgu0, wgu1 = _convert([np.asarray(ffn_w_gu[0], f32), np.asarray(ffn_w_gu[1], f32)], [norm_w[0, 1], norm_w[1, 1]])
    wd0, wd1 = _convert([np.asarray(ffn_w_down[0], f32), np.asarray(ffn_w_down[1], f32)], None)
    (win2,) = _convert([np.asarray(od_w_in[0], f32)], [norm_w[1, 0]])

    ncA = _get("A", build_stageA, Tc)
    insA = []
    for c in range(N_CORES):
        b, r = divmod(c, RPB)
        xh = np.zeros((Tc + 32, D), f32)
        xh[32:] = x[b, r * Tc:(r + 1) * Tc]
        if r > 0:
            xh[:32] = x[b, r * Tc - 32:r * Tc]
        insA.append(dict(xh=xh, win=win1, conv_w=np.ascontiguousarray(np.asarray(ev_conv_w[0], f32)[:, 0, :]),
                         conv_b=np.asarray(ev_conv_b[0], f32), ln_w=np.asarray(ev_conv_ln_w[0], f32),
                         ln_b=np.asarray(ev_conv_ln_b[0], f32), ident=ident))
    rA = _run(ncA, insA)
    rA = [{k: np.asarray(v) for k, v in r.items()} for r in rA]

    ncN = _get("N", build_nsa, Sq)
    insN = []
    for c in range(N_CORES):
        b, g = divmod(c, RPB)
        cores = [b * RPB + r for r in range(RPB)]
        cat = lambda key, rows: np.ascontiguousarray(np.concatenate([rA[cc][key][rows] for cc in cores], axis=1))
        catT = lambda key, cols: np.ascontiguousarray(np.concatenate([rA[cc][key][:, cols] for cc in cores], axis=0))
        gat = np.concatenate([rA[cc]["gates"] for cc in cores], axis=0).reshape(Sq, 3, 4, 4)[:, :, g, :].reshape(Sq, 12)
        d = dict(qT=cat("qT", slice(g * 256, (g + 1) * 256)), kcT=cat("kcT", slice(g * 64, (g + 1) * 64)),
                 vcT=cat("vcT", slice(g * 64, (g + 1) * 64)), ksT=cat("ksT", slice(g * 64, (g + 1) * 64)),
                 kwT=cat("kwT", slice(g * 64, (g + 1) * 64)), vs=catT("vs", slice(g * 64, (g + 1) * 64)),
                 vw=catT("vw", slice(g * 64, (g + 1) * 64)), gates=np.ascontiguousarray(gat),
                 cpos=np.asarray(ev_cmp_pos[0], f32), cw1=np.asarray(ev_cmp_w1[0], f32), cb1=np.asarray(ev_cmp_b1[0], f32),
                 cw2=np.asarray(ev_cmp_w2[0], f32), cb2=np.asarray(ev_cmp_b2[0], f32))
        d.update(nsa_consts(g, Sq))
        insN.append(d)
    rN = _run(ncN, insN)
    rN = [np.asarray(r["bT"]) for r in rN]

    ncB = _get("B", build_stageB, Tc, "mid")
    insB = []
    for c in range(N_CORES):
        b, r = divmod(c, RPB)
        ts = slice(r * Tc, (r + 1) * Tc)
        mixT = np.concatenate([rA[c]["aT"]] + [rN[b * RPB + g][:, ts] for g in range(4)], axis=0)
        insB.append(dict(x=np.ascontiguousarray(x[b, ts]), mixT=np.ascontiguousarray(mixT), wo=wo0, wgu=wgu0, wd=wd0, ident=ident,
                         win2=win2, gam=np.asarray(od_lb_gamma, f32)))
    del rA
    rB = _run(ncB, insB)
    rB = [{k: np.asarray(v) for k, v in r.items()} for r in rB]
    del insB

    HPC = 16 // RPB
    RW = HPC * 128
    ncH = _get("H", build_hgrn, Sq, HPC, 512)
    maskT = np.triu(np.ones((64, 64), f32))
    rmask = np.ascontiguousarray(np.broadcast_to((np.arange(512) % 64 != 0).astype(f32)[None], (128, 512)))
    insH = []
    for c in range(N_CORES):
        b, hq = divmod(c, RPB)
        cores = [b * RPB + r for r in range(RPB)]
        rows = slice(hq * RW, (hq + 1) * RW)
        cat = lambda key: np.ascontiguousarray(np.concatenate([rB[cc][key][rows] for cc in cores], axis=1))
        iv = np.ascontiguousarray(np.concatenate([rB[cc]["iv"][:, rows] for cc in cores], axis=0))
        insH.append(dict(qT=cat("qT"), kT=cat("kT"), lfT=cat("lfT"), iv=iv, gnw=np.ascontiguousarray(np.asarray(od_gnorm_w[0], f32)[rows]),
                         ident=ident, maskT=maskT, rmask=rmask))
    rH = _run(ncH, insH)
    rH = [np.asarray(r["oT"]) for r in rH]
    del insH

    ncF = _get("B", build_stageB, Tc, "fin")
    insF = []
    for c in range(N_CORES):
        b, r = divmod(c, RPB)
        ts = slice(r * Tc, (r + 1) * Tc)
        mixT = np.concatenate([rH[b * RPB + hq][:, ts] for hq in range(RPB)], axis=0)
        insF.append(dict(x=rB[c]["x2"], mixT=np.ascontiguousarray(mixT), sgT=rB[c]["sgTo"], wo=wo1, wgu=wgu1, wd=wd1, ident=ident,
                         fnw=np.asarray(final_norm_w, f32)))
    rF = _run(ncF, insF)
    out = np.empty((B, Sq, D), f32)
    for c in range(N_CORES):
        b, r = divmod(c, RPB)
        out[b, r * Tc:(r + 1) * Tc] = np.asarray(rF[c]["out"])
    return out
```
